# Optimizing a Trainium2 kernel written in Bass

```python
import math
import jax, jax.numpy as jnp
from jax import lax
import numpy as np

D_MODEL = 4096
BATCH = 32
SEQ = 256
DEPTH = 2
DEC_BATCH = 2
DEC_SEQ = 4096
PAST_LEN = 512

GRID_W = 64
D_MIX = D_MODEL
S5_WIDTH = D_MIX // 4
S5_CH = 16
S5_GROUPS = S5_WIDTH // S5_CH
S5_P = 64
GLA_WIDTH = 3 * D_MIX // 8
GLA_HEADS = 4
GLA_DK = GLA_WIDTH // (2 * GLA_HEADS)
GLA_DV = GLA_WIDTH // GLA_HEADS
GLA_RANK = 16
GLA_TAU = 16.0
HG_WIDTH = D_MIX - S5_WIDTH - GLA_WIDTH
HG_EXPAND = 128
HG_HEADS = HG_WIDTH // HG_EXPAND
HG_DV = HG_WIDTH // HG_HEADS
CHUNK = 32
D_FF = 11008
N_MOD = 9
EPS = 1e-6
F_FLOOR = 1e-30
IN_SPLITS = (S5_WIDTH, GLA_HEADS * GLA_DK, GLA_HEADS * GLA_DK, GLA_WIDTH, GLA_WIDTH, 2 * GLA_RANK, HG_WIDTH, HG_WIDTH, HG_WIDTH, HG_WIDTH, HG_WIDTH)
D_IN = S5_WIDTH + 2 * GLA_HEADS * GLA_DK + 2 * GLA_WIDTH + 2 * GLA_RANK + 5 * HG_WIDTH

kernel_name = 'hybrid_s5_gla_hgrn2_prefix_diffusion_step'


def rmsnorm(x, g):
    xf = x.astype(jnp.float32)
    y = xf * lax.rsqrt(jnp.mean(xf * xf, axis=-1, keepdims=True) + EPS)
    return (y * g.astype(jnp.float32)).astype(x.dtype)


def swiglu(h, w_gate, w_up, w_down):
    return (jax.nn.silu(h @ w_gate) * (h @ w_up)) @ w_down


def flip(t):
    return jnp.flip(t, axis=1)


def grid_to_colmajor(t):
    b, n = t.shape[:2]
    rows = n // GRID_W
    return t.reshape(b, rows, GRID_W, *t.shape[2:]).swapaxes(1, 2).reshape(t.shape)


def colmajor_to_grid(t):
    b, n = t.shape[:2]
    rows = n // GRID_W
    return t.reshape(b, GRID_W, rows, *t.shape[2:]).swapaxes(1, 2).reshape(t.shape)


def _cplx_combine(e1, e2):
    a1r, a1i, b1r, b1i = e1
    a2r, a2i, b2r, b2i = e2
    return (a2r * a1r - a2i * a1i, a2r * a1i + a2i * a1r,
            a2r * b1r - a2i * b1i + b2r, a2r * b1i + a2i * b1r + b2i)


def s5_scan(u, a_re, a_im, log_dt, b_re, b_im, c_re, c_im, h0_re, h0_im):
    dt = jnp.exp(log_dt)[:, None]
    mag = jnp.exp(a_re * dt)
    lam_re = mag * jnp.cos(a_im * dt)
    lam_im = mag * jnp.sin(a_im * dt)
    den = a_re * a_re + a_im * a_im
    z_re = ((lam_re - 1.0) * a_re + lam_im * a_im) / den
    z_im = (lam_im * a_re - (lam_re - 1.0) * a_im) / den
    bb_re = z_re[..., None] * b_re - z_im[..., None] * b_im
    bb_im = z_re[..., None] * b_im + z_im[..., None] * b_re
    bu_re = jnp.einsum('blgc,gpc->blgp', u, bb_re)
    bu_im = jnp.einsum('blgc,gpc->blgp', u, bb_im)
    bu_re = bu_re.at[:, 0].add(lam_re * h0_re - lam_im * h0_im)
    bu_im = bu_im.at[:, 0].add(lam_re * h0_im + lam_im * h0_re)
    ar = jnp.broadcast_to(lam_re, bu_re.shape)
    ai = jnp.broadcast_to(lam_im, bu_im.shape)
    _, _, x_re, x_im = lax.associative_scan(_cplx_combine, (ar, ai, bu_re, bu_im), axis=1)
    y = jnp.einsum('blgp,gcp->blgc', x_re, c_re) - jnp.einsum('blgp,gcp->blgc', x_im, c_im)
    return y, x_re[:, -1], x_im[:, -1]


def chunk_gla(q, k, v, log_g, h0):
    bsz, n, nh, _ = q.shape
    nc = n // CHUNK

    def to_chunks(t):
        return t.reshape(bsz, nc, CHUNK, nh, t.shape[-1]).transpose(1, 0, 3, 2, 4)

    causal = jnp.tril(jnp.ones((CHUNK, CHUNK), dtype=bool))[:, :, None]

    def step(s, blk):
        qc, kc, vc, gc = blk
        b = jnp.cumsum(gc, axis=2)
        diff = b[:, :, :, None, :] - b[:, :, None, :, :]
        decay = jnp.where(causal, jnp.exp(jnp.where(causal, diff, 0.0)), 0.0)
        attn = jnp.einsum('bhik,bhjk,bhijk->bhij', qc, kc, decay)
        b_last = b[:, :, -1:, :]
        o = (jnp.einsum('bhij,bhjv->bhiv', attn, vc)
             + jnp.einsum('bhik,bhkv->bhiv', qc * jnp.exp(b), s))
        s = (jnp.exp(b_last[:, :, 0, :])[..., None] * s
             + jnp.einsum('bhjk,bhjv->bhkv', kc * jnp.exp(b_last - b), vc))
        return s, o

    s_fin, o = lax.scan(step, h0, (to_chunks(q), to_chunks(k), to_chunks(v), to_chunks(log_g)))
    o = o.transpose(1, 0, 3, 2, 4).reshape(bsz, n, nh, v.shape[-1])
    return o, s_fin


def bidirectional_gla(q, k_f, k_b, v, lg_f, lg_b, h0):
    o_f, s_f = chunk_gla(q, k_f, v, lg_f, h0[:, 0])
    o_b, s_b = chunk_gla(flip(q), flip(k_b), flip(v), flip(lg_b), h0[:, 1])
    return o_f + flip(o_b), jnp.stack([s_f, s_b], axis=1)


def head_norm_gate(o, g_norm, gate):
    o = o * lax.rsqrt(jnp.mean(o * o, axis=-1, keepdims=True) + EPS) * g_norm.astype(jnp.float32)
    return o.reshape(o.shape[0], o.shape[1], -1) * jax.nn.silu(gate.astype(jnp.float32))


def s5_dir_params(lw, d):
    f32 = jnp.float32
    return tuple(lw[name][d].astype(f32) for name in ('s5_a_re', 's5_a_im', 's5_log_dt', 's5_b_re', 's5_b_im', 's5_c_re', 's5_c_im'))


def token_mixer(h, st, lw, latent):
    f32 = jnp.float32
    bsz, n = h.shape[:2]
    split_at = [int(s) for s in np.cumsum(IN_SPLITS)[:-1]]
    (u_a, q_b, k_b, v_b, g_b, lr_b, q_c, zf_c, zb_c, i_c, g_c) = jnp.split(h @ lw['w_in'], split_at, axis=-1)
    s5_re0, s5_im0, gla0, hg0 = (s.astype(f32) for s in st)

    u = u_a.astype(f32).reshape(bsz, n, S5_GROUPS, S5_CH)
    y_f, fre_f, fim_f = s5_scan(u, *s5_dir_params(lw, 0), s5_re0[:, 0], s5_im0[:, 0])
    y_b, fre_b, fim_b = s5_scan(flip(u), *s5_dir_params(lw, 1), s5_re0[:, 1], s5_im0[:, 1])
    y_a = jax.nn.gelu((y_f + flip(y_b) + lw['s5_d'].astype(f32) * u).reshape(bsz, n, S5_WIDTH))
    out_a = y_a * jax.nn.sigmoid(y_a @ lw['s5_glu_w'].astype(f32) + lw['s5_glu_b'].astype(f32))

    qb = q_b.astype(f32).reshape(bsz, n, GLA_HEADS, GLA_DK) * GLA_DK ** -0.5
    kb = k_b.astype(f32).reshape(bsz, n, GLA_HEADS, GLA_DK)
    vb = v_b.astype(f32).reshape(bsz, n, GLA_HEADS, GLA_DV)
    lr = lr_b.astype(f32)
    w2 = lw['gla_w2'].astype(f32)
    b2 = lw['gla_b2'].astype(f32)
    lg = [(jax.nn.log_sigmoid(lr[..., d * GLA_RANK:(d + 1) * GLA_RANK] @ w2[d] + b2[d]) / GLA_TAU).reshape(bsz, n, GLA_HEADS, GLA_DK) for d in range(2)]
    o_b, fin_gla = bidirectional_gla(qb, kb, kb, vb, lg[0], lg[1], gla0)
    out_b = head_norm_gate(o_b, lw['gla_norm_g'], g_b)

    qc = q_c.astype(f32).reshape(bsz, n, HG_HEADS, HG_EXPAND) * HG_EXPAND ** -0.5
    vc = i_c.astype(f32).reshape(bsz, n, HG_HEADS, HG_DV)
    lb = lw['hg_lb']
    log_f, keys = [], []
    for d, z in enumerate((zf_c, zb_c)):
        f_gate = lb[d] + (1.0 - lb[d]) * jax.nn.sigmoid(z.astype(f32))
        log_f.append(jnp.log(jnp.maximum(f_gate, F_FLOOR)).reshape(bsz, n, HG_HEADS, HG_EXPAND))
        keys.append((1.0 - f_gate).reshape(bsz, n, HG_HEADS, HG_EXPAND))
    seqs = (qc, keys[0], keys[1], vc, log_f[0], log_f[1])
    if latent:
        seqs = tuple(grid_to_colmajor(t) for t in seqs)
    o_c, fin_hg = bidirectional_gla(*seqs, hg0)
    if latent:
        o_c = colmajor_to_grid(o_c)
    out_c = head_norm_gate(o_c, lw['hg_norm_g'], g_c)

    mixed = jnp.concatenate([out_a, out_b, out_c], axis=-1).astype(h.dtype)
    new_st = (jnp.stack([fre_f, fre_b], axis=1), jnp.stack([fim_f, fim_b], axis=1), fin_gla, fin_hg)
    return mixed @ lw['w_out'], new_st


def trunk_layer(x, cond, st, lw, latent):
    m = (jax.nn.silu(cond) @ lw['ada_w'] + lw['ada_b'])[:, None, :].astype(x.dtype)
    sh1, sc1, g1, sh2, sc2, g2, sh3, sc3, g3 = jnp.split(m, N_MOD, axis=-1)
    ng = lw['norm_g']
    h = rmsnorm(x, ng[0]) * (1.0 + sc1) + sh1
    x = x + 0.5 * g1 * swiglu(h, lw['ffn1_wg'], lw['ffn1_wu'], lw['ffn1_wd'])
    h = rmsnorm(x, ng[1]) * (1.0 + sc2) + sh2
    mix, new_st = token_mixer(h, st, lw, latent)
    x = x + g2 * mix
    h = rmsnorm(x, ng[2]) * (1.0 + sc3) + sh3
    x = x + 0.5 * g3 * swiglu(h, lw['ffn2_wg'], lw['ffn2_wu'], lw['ffn2_wd'])
    return x, new_st


def setup_inputs(seed: int = 0) -> dict:
    key = jax.random.key(seed)
    ks = list(jax.random.split(key, 48))
    f32 = jnp.float32

    def nrm(shape, scale=1.0):
        return scale * jax.random.normal(ks.pop(), shape, f32)

    def gain(shape):
        return 1.0 + 0.02 * jax.random.normal(ks.pop(), shape, f32)

    G, P, CH = S5_GROUPS, S5_P, S5_CH
    inp = {}
    inp['x_prompt'] = nrm((BATCH, SEQ, D_MODEL))
    inp['x_sample'] = nrm((DEC_BATCH, DEC_SEQ, D_MODEL))
    inp['state_s5_re'] = nrm((DEC_BATCH, DEPTH, 2, G, P), 0.3)
    inp['state_s5_im'] = nrm((DEC_BATCH, DEPTH, 2, G, P), 0.3)
    inp['state_gla'] = nrm((DEC_BATCH, DEPTH, 2, GLA_HEADS, GLA_DK, GLA_DV))
    inp['state_hgrn'] = nrm((DEC_BATCH, DEPTH, 2, HG_HEADS, HG_EXPAND, HG_DV), 0.5)
    inp['c'] = nrm((DEC_BATCH, D_MODEL))
    inp['c_ctx'] = nrm((D_MODEL,))
    inp['ada_w'] = nrm((DEPTH, D_MODEL, N_MOD * D_MODEL), 0.5 * D_MODEL ** -0.5)
    inp['ada_b'] = nrm((DEPTH, N_MOD * D_MODEL), 0.02)
    inp['norm_g'] = gain((DEPTH, 3, D_MODEL))
    inp['ffn1_wg'] = nrm((DEPTH, D_MODEL, D_FF), D_MODEL ** -0.5)
    inp['ffn1_wu'] = nrm((DEPTH, D_MODEL, D_FF), D_MODEL ** -0.5)
    inp['ffn1_wd'] = nrm((DEPTH, D_FF, D_MODEL), D_FF ** -0.5)
    inp['ffn2_wg'] = nrm((DEPTH, D_MODEL, D_FF), D_MODEL ** -0.5)
    inp['ffn2_wu'] = nrm((DEPTH, D_MODEL, D_FF), D_MODEL ** -0.5)
    inp['ffn2_wd'] = nrm((DEPTH, D_FF, D_MODEL), D_FF ** -0.5)
    inp['w_in'] = nrm((DEPTH, D_MODEL, D_IN), D_MODEL ** -0.5)
    inp['w_out'] = nrm((DEPTH, D_MIX, D_MODEL), D_MIX ** -0.5)
    inp['s5_a_re'] = -0.5 + nrm((DEPTH, 2, G, P), 0.01)
    inp['s5_a_im'] = math.pi * jnp.arange(P, dtype=f32) + nrm((DEPTH, 2, G, P), 0.01)
    inp['s5_log_dt'] = jax.random.uniform(ks.pop(), (DEPTH, 2, G), f32, math.log(1e-3), math.log(1e-1))
    inp['s5_b_re'] = nrm((DEPTH, 2, G, P, CH), (2.0 * CH) ** -0.5)
    inp['s5_b_im'] = nrm((DEPTH, 2, G, P, CH), (2.0 * CH) ** -0.5)
    inp['s5_c_re'] = nrm((DEPTH, 2, G, CH, P), (2.0 * P) ** -0.5)
    inp['s5_c_im'] = nrm((DEPTH, 2, G, CH, P), (2.0 * P) ** -0.5)
    inp['s5_d'] = nrm((DEPTH, G, CH))
    inp['s5_glu_w'] = nrm((DEPTH, S5_WIDTH, S5_WIDTH), S5_WIDTH ** -0.5)
    inp['s5_glu_b'] = nrm((DEPTH, S5_WIDTH), 0.02)
    inp['gla_w2'] = nrm((DEPTH, 2, GLA_RANK, GLA_HEADS * GLA_DK), GLA_RANK ** -0.5)
    inp['gla_b2'] = nrm((DEPTH, 2, GLA_HEADS * GLA_DK), 0.02)
    inp['gla_norm_g'] = gain((DEPTH, GLA_DV))
    inp['hg_lb_raw'] = nrm((2, DEPTH, HG_WIDTH), 0.5)
    inp['hg_norm_g'] = gain((DEPTH, HG_DV))
    inp['final_norm_g'] = gain((D_MODEL,))
    return inp


def reference(x_prompt, x_sample, state_s5_re, state_s5_im, state_gla, state_hgrn, c,
              c_ctx, ada_w, ada_b, norm_g, ffn1_wg, ffn1_wu, ffn1_wd, ffn2_wg, ffn2_wu, ffn2_wd,
              w_in, w_out, s5_a_re, s5_a_im, s5_log_dt, s5_b_re, s5_b_im, s5_c_re, s5_c_im, s5_d,
              s5_glu_w, s5_glu_b, gla_w2, gla_b2, gla_norm_g, hg_lb_raw, hg_norm_g, final_norm_g):
    f32 = jnp.float32
    lb_p = jax.nn.softmax(hg_lb_raw.astype(f32), axis=1)
    hg_lb = jnp.cumsum(lb_p, axis=1) - lb_p[:, :1]

    def layer_weights(l):
        return {'ada_w': ada_w[l], 'ada_b': ada_b[l], 'norm_g': norm_g[l],
                'ffn1_wg': ffn1_wg[l], 'ffn1_wu': ffn1_wu[l], 'ffn1_wd': ffn1_wd[l],
                'ffn2_wg': ffn2_wg[l], 'ffn2_wu': ffn2_wu[l], 'ffn2_wd': ffn2_wd[l],
                'w_in': w_in[l], 'w_out': w_out[l],
                's5_a_re': s5_a_re[l], 's5_a_im': s5_a_im[l], 's5_log_dt': s5_log_dt[l],
                's5_b_re': s5_b_re[l], 's5_b_im': s5_b_im[l], 's5_c_re': s5_c_re[l], 's5_c_im': s5_c_im[l],
                's5_d': s5_d[l], 's5_glu_w': s5_glu_w[l], 's5_glu_b': s5_glu_b[l],
                'gla_w2': gla_w2[l], 'gla_b2': gla_b2[l], 'gla_norm_g': gla_norm_g[l],
                'hg_lb': hg_lb[:, l], 'hg_norm_g': hg_norm_g[l]}

    bp = x_prompt.shape[0]
    zero_st = (jnp.zeros((bp, 2, S5_GROUPS, S5_P), f32), jnp.zeros((bp, 2, S5_GROUPS, S5_P), f32),
               jnp.zeros((bp, 2, GLA_HEADS, GLA_DK, GLA_DV), f32),
               jnp.zeros((bp, 2, HG_HEADS, HG_EXPAND, HG_DV), f32))
    xc = x_prompt
    ctx_states = []
    for l in range(DEPTH):
        xc, st = trunk_layer(xc, c_ctx[None, :], zero_st, layer_weights(l), False)
        ctx_states.append(st)
    y_prompt = rmsnorm(xc, final_norm_g)
    new_s5_re = jnp.stack([s[0] for s in ctx_states], axis=1)
    new_s5_im = jnp.stack([s[1] for s in ctx_states], axis=1)
    new_gla = jnp.stack([s[2] for s in ctx_states], axis=1)
    new_hgrn = jnp.stack([s[3] for s in ctx_states], axis=1)

    xs = x_sample
    for l in range(DEPTH):
        st = (state_s5_re[:, l], state_s5_im[:, l], state_gla[:, l], state_hgrn[:, l])
        xs, _ = trunk_layer(xs, c, st, layer_weights(l), True)
    y_sample = rmsnorm(xs, final_norm_g)
    return (y_prompt, y_sample, new_s5_re, new_s5_im, new_gla, new_hgrn)
```

```python
import numpy as np
from contextlib import ExitStack
import concourse.bass as bass
import concourse.mybir as mybir
from concourse.bass_utils import run_bass_kernel_spmd

F32 = mybir.dt.float32
BF16 = mybir.dt.bfloat16
AF = mybir.ActivationFunctionType
ALU = mybir.AluOpType

EPOCH = [30000]
FULL_CFG = dict(D=4096, DFF=11008, NP=4, SEQ=256, LS=4096, DEPTH=2, TB=512, MIX=True)


class Buf:
    __slots__ = ("w", "r", "name", "excl")

    def __init__(self, name="", excl=False):
        self.w = {}
        self.r = {}
        self.name = name
        self.excl = excl


class Eng:
    def __init__(self, name, eng, sem):
        self.name, self.eng, self.sem = name, eng, sem
        self.cnt = 0
        self.own = {id(sem)}
        self.prev = []
        self.nep = 0
        self.waited = {}
        self.pending = []
        self.pool = []
        self.ndma = 0


class Ctx:
    def __init__(self, nc, es):
        self.nc = nc
        self.es = es
        self.E = {}
        for nm, e in (("pe", nc.tensor), ("act", nc.scalar), ("dve", nc.vector),
                      ("pool", nc.gpsimd), ("sp", nc.sync)):
            self.E[nm] = Eng(nm, e, es.enter_context(nc.semaphore("s_" + nm)))
        for nm, n in (("sp", 40), ("pool", 24), ("act", 12)):
            self.E[nm].pool = [[es.enter_context(nc.semaphore("d_%s%d" % (nm, i))), 0] for i in range(n)]
        self.nbuf = 0
        self.nsem = 5 + 40 + 24 + 12

    def buf(self, name=""):
        return Buf(name)

    def sb(self, name, shape, dt):
        t = self.es.enter_context(self.nc.sbuf_tensor(name, list(shape), dt))
        return t

    def _wait(self, E, deps):
        for sem, val in deps:
            if E.name == "pe" and id(sem) in E.own:
                continue
            key = id(sem)
            if E.waited.get(key, 0) < val:
                E.eng.wait_ge(sem, val)
                E.waited[key] = val

    def _deps(self, r, w):
        deps = {}
        for b in r:
            for s, v in b.w.values():
                k = id(s)
                if k not in deps or deps[k][1] < v:
                    deps[k] = (s, v)
            if b.excl:
                for s, v in b.r.values():
                    k = id(s)
                    if k not in deps or deps[k][1] < v:
                        deps[k] = (s, v)
        for b in w:
            for d in (b.w, b.r):
                for s, v in d.values():
                    k = id(s)
                    if k not in deps or deps[k][1] < v:
                        deps[k] = (s, v)
        return list(deps.values())

    def _record(self, ev, r, w):
        s, v = ev
        k = id(s)
        for b in r:
            b.r[k] = ev
        for b in w:
            b.w = {k: ev}
            b.r = {}

    def op(self, en, fn, r=(), w=(), sig=True):
        E = self.E[en]
        self._wait(E, self._deps(r, w))
        ins = fn(E.eng)
        if sig:
            if E.cnt >= EPOCH[0]:
                E.prev.append((E.sem, E.cnt))
                E.nep += 1
                self.nsem += 1
                assert self.nsem <= 104, "too many semaphores"
                E.sem = self.es.enter_context(self.nc.semaphore("s_%s_e%d" % (E.name, E.nep)))
                E.own.add(id(E.sem))
                E.cnt = 0
            ins.then_inc(E.sem, 1)
            E.cnt += 1
            ev = (E.sem, E.cnt)
            self._record(ev, r, w)
            for (pr, pw) in E.pending:
                self._record(ev, pr, pw)
            E.pending = []
        else:
            E.pending.append((list(r), list(w)))
        return ins

    def dma(self, out, in_, r=(), w=(), q="sp"):
        E = self.E[q]
        slot = E.pool[E.ndma % len(E.pool)]
        E.ndma += 1
        deps = self._deps(r, w)
        if slot[1] > 0:
            deps.append((slot[0], slot[1]))
        self._wait(E, deps)
        slot[1] += 16
        E.eng.dma_start(out=out, in_=in_).then_inc(slot[0], 16)
        self._record((slot[0], slot[1]), r, w)

    def barrier(self):
        evs = []
        for E in self.E.values():
            if E.cnt > 0:
                evs.append((E.sem, E.cnt))
            for pv in E.prev[-1:]:
                evs.append(pv)
            for s, v in E.pool:
                if v > 0:
                    evs.append((s, v))
        for E in self.E.values():
            self._wait(E, evs)


def col_tiles_win(D):
    S5W = D // 4
    GW = 3 * D // 8
    HW = D - S5W - GW
    GH = 4
    GK = GW // (2 * GH)
    splits = [S5W, GH * GK, GH * GK, GW, GW, 32, HW, HW, HW, HW, HW]
    names = ["u", "gq", "gk", "gv", "gg", "lr", "hq", "hzf", "hzb", "hi", "hg"]
    tiles = []
    off = 0
    for nm, wd in zip(names, splits):
        o = 0
        while o < wd:
            wdt = min(128, wd - o)
            tiles.append((off + o, wdt, nm, o))
            o += wdt
        off += wd
    return tiles, dict(zip(names, splits))


def build_program(cfg):
    D, DFF, NP, SEQ, LS, DEPTH, TB = (cfg[k] for k in ("D", "DFF", "NP", "SEQ", "LS", "DEPTH", "TB"))
    MIX = cfg.get("MIX", True)
    KT = D // 128
    FT = DFF // 128
    TOKP = NP * SEQ
    TOK = TOKP + LS
    NBLK = TOK // TB
    NTT = TOK // 128
    assert TOKP % TB == 0 and LS % TB == 0 and DFF % 128 == 0
    D_IN = D // 4 + 2 * (3 * D // 8) // 2 + 2 * (3 * D // 8) + 32 + 5 * (D - D // 4 - 3 * D // 8)

    nc = bass.Bass("TRN2", target_bir_lowering=False)

    _ucnt = [0]

    def sbt(name, shape, dt):
        _ucnt[0] += 1
        return nc.sbuf_tensor("%s_%d" % (name, _ucnt[0]), shape, dt)

    def din(name, shape):
        return nc.dram_tensor(name, list(shape), F32, kind="ExternalInput").ap()

    def dscr(name, shape, dt):
        return nc.dram_tensor(name, list(shape), dt, kind="Internal").ap()

    x_in = din("x_in", [TOK, D])
    cond = din("cond", [2, D])
    ada_w = din("ada_w", [DEPTH, D, 9 * D])
    ada_b = din("ada_b", [DEPTH, 9 * D])
    norm_g = din("norm_g", [DEPTH, 3, D])
    fw = {}
    for nm, shp in (("ffn1_wg", [DEPTH, D, DFF]), ("ffn1_wu", [DEPTH, D, DFF]), ("ffn1_wd", [DEPTH, DFF, D]),
                    ("ffn2_wg", [DEPTH, D, DFF]), ("ffn2_wu", [DEPTH, D, DFF]), ("ffn2_wd", [DEPTH, DFF, D]),
                    ("w_in", [DEPTH, D, D_IN]), ("w_out", [DEPTH, D, D])):
        fw[nm] = din(nm, shp)
    final_g = din("final_norm_g", [D])
    c_ident = din("c_ident", [128, 128])
    c_triu = din("c_triu", [128, 128])
    c_tril = din("c_tril", [128, 128])
    _S5W = D // 4
    _GW = 3 * D // 8
    _HW = D - _S5W - _GW
    _GK, _GV, _HH = _GW // 8, _GW // 4, _HW // 128
    st_gla = din("st_gla", [DEPTH, 2, 4, _GK, _GV])
    st_hgrn = din("st_hgrn", [DEPTH, 2, _HH, 128, 128])
    gla_w2 = din("gla_w2", [DEPTH, 2, 16, 4 * _GK])
    gla_b2 = din("gla_b2", [DEPTH, 2, 4 * _GK])
    gla_norm_g = din("gla_norm_g", [DEPTH, _GV])
    hg_lb_raw = din("hg_lb_raw", [2, DEPTH, _HW])
    hg_norm_g = din("hg_norm_g", [DEPTH, 128])
    _NG = _S5W // 16
    st_s5re = din("st_s5re", [DEPTH, 2, _NG * 64])
    st_s5im = din("st_s5im", [DEPTH, 2, _NG * 64])
    s5_a_re = din("s5_a_re", [DEPTH, 2, _NG, 64])
    s5_a_im = din("s5_a_im", [DEPTH, 2, _NG, 64])
    s5_log_dt = din("s5_log_dt", [DEPTH, 2, _NG])
    s5_b_re = din("s5_b_re", [DEPTH, 2, _NG, 64, 16])
    s5_b_im = din("s5_b_im", [DEPTH, 2, _NG, 64, 16])
    s5_c_re = din("s5_c_re", [DEPTH, 2, _NG, 16, 64])
    s5_c_im = din("s5_c_im", [DEPTH, 2, _NG, 16, 64])
    s5_d = din("s5_d", [DEPTH, _S5W])
    s5_glu_w = din("s5_glu_w", [DEPTH, _S5W, _S5W])
    s5_glu_b = din("s5_glu_b", [DEPTH, _S5W])
    out_s5re = nc.dram_tensor("out_s5re", [NP, DEPTH, 2, _NG, 64], F32, kind="ExternalOutput").ap()
    out_s5im = nc.dram_tensor("out_s5im", [NP, DEPTH, 2, _NG, 64], F32, kind="ExternalOutput").ap()
    out_gla = nc.dram_tensor("out_gla", [NP, DEPTH, 2, 4, _GK, _GV], F32, kind="ExternalOutput").ap()
    out_hgrn = nc.dram_tensor("out_hgrn", [NP, DEPTH, 2, _HH, 128, 128], F32, kind="ExternalOutput").ap()
    y_out = nc.dram_tensor("y_out", [TOK, D], F32, kind="ExternalOutput").ap()

    xT = dscr("xT", [D, TOK], F32)
    wt = {}
    for l in range(DEPTH):
        for f in ("ffn1", "ffn2"):
            wt[(l, f + "_wg")] = dscr("t_%s_wg%d" % (f, l), [FT, 128, KT * 128], BF16)
            wt[(l, f + "_wu")] = dscr("t_%s_wu%d" % (f, l), [FT, 128, KT * 128], BF16)
            wt[(l, f + "_wd")] = dscr("t_%s_wd%d" % (f, l), [KT, 128, FT * 128], BF16)

    with ExitStack() as es:
        cx = Ctx(nc, es)
        op, dma = cx.op, cx.dma

        ident = cx.sb("ident", [128, 128], F32)
        b_ident = cx.buf()
        dma(ident[:, :], c_ident[:, :], w=[b_ident])
        ones_f = cx.sb("ones_f", [128, 128], F32)
        ones_b = cx.sb("ones_b", [128, 128], BF16)
        b_ones = cx.buf()
        op("pool", lambda e: e.memset(ones_f[:, :], 1.0), w=[b_ones])
        op("pool", lambda e: e.memset(ones_b[:, :], 1.0), w=[b_ones])
        psum = [es.enter_context(nc.psum_tensor("ps%d" % i, [128, 512], F32)) for i in range(8)]
        b_ps = [Buf("ps%d" % i, excl=True) for i in range(8)]
        ps_rr = [0]

        def next_ps():
            i = ps_rr[0] % 8
            ps_rr[0] += 1
            return psum[i], b_ps[i]

        modc = cx.sb("modc", [128, 9 * KT * 2], F32)
        b_modc = cx.buf()
        gcol = cx.sb("gcol", [128, 4 * KT], F32)
        b_gcol = cx.buf()
        Acol = cx.sb("Acol", [128, 3 * KT * 2], F32)
        Gcol = cx.sb("Gcol", [128, 3 * KT * 2], F32)
        b_AG = cx.buf()

        def mod_ap(m, kt, j):
            i = ((m * KT) + kt) * 2 + j
            return modc[:, i:i + 1]

        _rowcache = {}

        def colize(es2, row_ap, n, out_tile, out_off, b_out, tag, rd=()):
            if not hasattr(es2, "_rowc"):
                es2._rowc = (es2.enter_context(sbt("row_" + tag, [1, D], F32)), cx.buf())
            rowt, b_row = es2._rowc
            assert n <= D
            dma(rowt[0:1, 0:n], row_ap, r=list(rd), w=[b_row])
            nt = n // 128
            ps, bp = next_ps()
            for t in range(nt):
                op("pe", lambda e, t=t: e.matmul(ps[:, t:t + 1], rowt[0:1, t * 128:(t + 1) * 128],
                                                ones_f[0:1, 0:1], start=True, stop=True),
                   r=[b_row, b_ones], w=[bp], sig=(t == nt - 1))
            op("dve", lambda e: e.tensor_copy(out_tile[:, out_off:out_off + nt], ps[:, 0:nt]),
               r=[bp], w=[b_out])

        def precast(src2d, Krows, Ncols, dst, tagn):
            kts = Krows // 128
            with ExitStack() as es2:
                stg = [es2.enter_context(sbt("pc_s%d" % i, [128, kts * 128], F32)) for i in range(2)]
                outb = [es2.enter_context(sbt("pc_o%d" % i, [128, kts * 128], BF16)) for i in range(2)]
                b_s = [cx.buf(), cx.buf()]
                b_o = [cx.buf(), cx.buf()]
                srcv = src2d.rearrange("(kt p) n -> p kt n", p=128)
                engs = ["act", "dve", "pool"]
                for ct in range(Ncols // 128):
                    i = ct % 2
                    s3 = stg[i][:, :].rearrange("p (kt c) -> p kt c", c=128)
                    half = kts // 2
                    dma(s3[:, 0:half, :], srcv[:, 0:half, ct * 128:(ct + 1) * 128], w=[b_s[i]])
                    dma(s3[:, half:kts, :], srcv[:, half:kts, ct * 128:(ct + 1) * 128], w=[b_s[i]])
                    en = engs[ct % 3]
                    if en == "act":
                        op("act", lambda e, i=i: e.activation(out=outb[i][:, :], in_=stg[i][:, :], func=AF.Copy),
                           r=[b_s[i]], w=[b_o[i]])
                    else:
                        op(en, lambda e, i=i: e.tensor_copy(outb[i][:, :], stg[i][:, :]), r=[b_s[i]], w=[b_o[i]])
                    dma(dst[ct, :, :], outb[i][:, :], r=[b_o[i]], q="pool")
            cx.barrier()

        for l in range(DEPTH):
            for f in ("ffn1", "ffn2"):
                precast(fw[f + "_wg"][l], D, DFF, wt[(l, f + "_wg")], "g")
                precast(fw[f + "_wu"][l], D, DFF, wt[(l, f + "_wu")], "u")
                precast(fw[f + "_wd"][l], DFF, D, wt[(l, f + "_wd")], "d")

        with ExitStack() as es2:
            xin = [es2.enter_context(sbt("xin%d" % i, [128, D], F32)) for i in range(2)]
            xo = [es2.enter_context(sbt("xo%d" % i, [128, D], F32)) for i in range(2)]
            b_xin = [cx.buf(), cx.buf()]
            b_xo = [cx.buf(), cx.buf()]
            xTv = xT.rearrange("(kt p) t -> p kt t", p=128)
            for tt in range(NTT):
                i = tt % 2
                dma(xin[i][:, :], x_in[tt * 128:(tt + 1) * 128, :], w=[b_xin[i]])
                for k4 in range(KT // 4):
                    ps, bp = next_ps()
                    for j in range(4):
                        kt = k4 * 4 + j
                        op("pe", lambda e, kt=kt, j=j, ps=ps: e.transpose(ps[:, j * 128:(j + 1) * 128],
                                                                           xin[i][:, kt * 128:(kt + 1) * 128], ident[:, :]),
                           r=[b_xin[i], b_ident], w=[bp], sig=(j == 3))
                    en = "act" if k4 % 2 == 0 else "dve"
                    if en == "act":
                        op("act", lambda e, k4=k4, ps=ps: e.activation(out=xo[i][:, k4 * 512:(k4 + 1) * 512], in_=ps[:, :], func=AF.Copy),
                           r=[bp], w=[b_xo[i]])
                    else:
                        op("dve", lambda e, k4=k4, ps=ps: e.tensor_copy(xo[i][:, k4 * 512:(k4 + 1) * 512], ps[:, :]),
                           r=[bp], w=[b_xo[i]])
                dma(xTv[:, :, tt * 128:(tt + 1) * 128], xo[i][:, :].rearrange("p (kt t) -> p kt t", t=128),
                    r=[b_xo[i]], q="pool")
        cx.barrier()

        with ExitStack() as es2:
            colize(es2, final_g.rearrange("(o n) -> o n", o=1), D, gcol, 3 * KT, b_gcol, "fg")
        cx.barrier()

        def ada_phase(l):
            with ExitStack() as es2:
                craw = es2.enter_context(sbt("craw", [128, 2 * KT], F32))
                scT = es2.enter_context(sbt("scT", [128, 2 * KT], F32))
                b_craw = cx.buf()
                b_scT = cx.buf()
                for j in range(2):
                    colize(es2, cond[j:j + 1, :], D, craw, j * KT, b_craw, "c%d" % j)
                sv = scT[:, :].rearrange("p (kt j) -> p j kt", j=2)
                for j in range(2):
                    op("act", lambda e, j=j: e.activation(out=sv[:, j, :], in_=craw[:, j * KT:(j + 1) * KT], func=AF.Silu),
                       r=[b_craw], w=[b_scT])
                for n in range(3):
                    colize(es2, norm_g[l, n:n + 1, :], D, gcol, n * KT, b_gcol, "g%d" % n)
                pan = [es2.enter_context(sbt("apan%d" % i, [128, KT * 128], F32)) for i in range(3)]
                b_pan = [cx.buf() for _ in range(3)]
                brow = [es2.enter_context(sbt("abr%d" % i, [1, 128], F32)) for i in range(3)]
                b_brow = [cx.buf() for _ in range(3)]
                wv = ada_w[l].rearrange("(kt p) n -> p kt n", p=128)
                NFT = 9 * KT
                ps, bp = None, None
                for ft in range(NFT):
                    i = ft % 3
                    p3 = pan[i][:, :].rearrange("p (kt c) -> p kt c", c=128)
                    half = KT // 2
                    dma(p3[:, 0:half, :], wv[:, 0:half, ft * 128:(ft + 1) * 128], w=[b_pan[i]])
                    dma(p3[:, half:KT, :], wv[:, half:KT, ft * 128:(ft + 1) * 128], w=[b_pan[i]])
                    dma(brow[i][0:1, :], ada_b[l:l + 1, ft * 128:(ft + 1) * 128], w=[b_brow[i]])
                    if ft % 128 == 0:
                        ps, bp = next_ps()
                        ps0 = ft
                    o = (ft - ps0) * 2
                    for kt in range(KT):
                        op("pe", lambda e, kt=kt, i=i, o=o, ps=ps: e.matmul(ps[:, o:o + 2], pan[i][:, kt * 128:(kt + 1) * 128],
                                                                           scT[:, kt * 2:kt * 2 + 2], start=(kt == 0), stop=False),
                           r=[b_pan[i], b_scT], w=[bp], sig=False)
                    op("pe", lambda e, i=i, o=o, ps=ps: e.matmul(ps[:, o:o + 2], brow[i][0:1, :], ones_f[0:1, 0:2],
                                                                start=False, stop=True),
                       r=[b_brow[i], b_ones], w=[bp])
                    if ft % 128 == 127 or ft == NFT - 1:
                        n = ft - ps0 + 1
                        op("dve", lambda e, ps=ps, ps0=ps0, n=n: e.tensor_copy(modc[:, ps0 * 2:(ps0 + n) * 2], ps[:, 0:n * 2]),
                           r=[bp], w=[b_modc])
                for n in range(3):
                    for j in range(2):
                        sc = modc[:, :].rearrange("p (m kt j) -> p m j kt", kt=KT, j=2)[:, 3 * n + 1, j, :]
                        gt = modc[:, :].rearrange("p (m kt j) -> p m j kt", kt=KT, j=2)[:, 3 * n + 2, j, :]
                        Av = Acol[:, :].rearrange("p (n kt j) -> p n j kt", kt=KT, j=2)[:, n, j, :]
                        Gv = Gcol[:, :].rearrange("p (n kt j) -> p n j kt", kt=KT, j=2)[:, n, j, :]
                        op("dve", lambda e, sc=sc, Av=Av, n=n: e.scalar_tensor_tensor(Av, sc, 1.0, gcol[:, n * KT:(n + 1) * KT],
                                                                                      ALU.add, ALU.mult),
                           r=[b_modc, b_gcol], w=[b_AG])
                        coef = 1.0 if n == 1 else 0.5
                        op("dve", lambda e, gt=gt, Gv=Gv, coef=coef: e.tensor_scalar(Gv, gt, coef, None, ALU.mult),
                           r=[b_modc], w=[b_AG])
            cx.barrier()

        def A_ap(n, kt, j):
            i = (n * KT + kt) * 2 + j
            return Acol[:, i:i + 1]

        def G_ap(n, kt, j):
            i = (n * KT + kt) * 2 + j
            return Gcol[:, i:i + 1]

        def norm_scratch(es2):
            rstd = es2.enter_context(sbt("rstd", [128, TB], F32))
            sq = [es2.enter_context(sbt("sq%d" % i, [128, TB], BF16)) for i in range(2)]
            return rstd, sq, [cx.buf(), cx.buf()], cx.buf()

        def make_hT(nsc, blk, n, hT, b_hT, xbufs, b_x, tmpf, b_tmp, gain_only=None):
            j = 0 if blk * TB < TOKP else 1
            t0 = blk * TB
            xTv = xT.rearrange("(kt p) t -> p kt t", p=128)
            rstd, sq, b_sq, b_rstd = nsc
            ps, bp = next_ps()
            nx = len(xbufs)
            for kt in range(KT):
                i = kt % nx
                dma(xbufs[i][:, 0:TB], xTv[:, kt, t0:t0 + TB], w=[b_x[i]])
                s = kt % 2
                op("act", lambda e, i=i, s=s: e.activation(out=sq[s][:, :], in_=xbufs[i][:, 0:TB], func=AF.Square),
                   r=[b_x[i]], w=[b_sq[s]])
                op("pe", lambda e, s=s, kt=kt, ps=ps: e.matmul(ps[:, 0:TB], ones_b[:, :], sq[s][:, :], start=(kt == 0), stop=(kt == KT - 1)),
                   r=[b_sq[s], b_ones], w=[bp], sig=True)
            op("dve", lambda e, ps=ps: e.tensor_scalar(rstd[:, :], ps[:, 0:TB], 1.0 / D, 1e-6, ALU.mult, ALU.add),
               r=[bp], w=[b_rstd])
            op("act", lambda e: e.activation(out=rstd[:, :], in_=rstd[:, :], func=AF.Sqrt), r=[b_rstd], w=[b_rstd])
            op("dve", lambda e: e.reciprocal(rstd[:, :], rstd[:, :]), r=[b_rstd], w=[b_rstd])
            for kt in range(KT):
                i = kt % nx
                dma(xbufs[i][:, 0:TB], xTv[:, kt, t0:t0 + TB], w=[b_x[i]])
                s = kt % 2
                if gain_only is not None:
                    sc_ap = gcol[:, gain_only * KT + kt:gain_only * KT + kt + 1]
                    op("dve", lambda e, i=i, kt=kt, sc_ap=sc_ap: e.scalar_tensor_tensor(hT[:, kt, :], xbufs[i][:, 0:TB], sc_ap, rstd[:, :],
                                                                                       ALU.mult, ALU.mult),
                       r=[b_x[i], b_rstd, b_gcol], w=[b_hT])
                else:
                    op("dve", lambda e, i=i, kt=kt, s=s: e.scalar_tensor_tensor(tmpf[s][:, :], xbufs[i][:, 0:TB], A_ap(n, kt, j), rstd[:, :],
                                                                              ALU.mult, ALU.mult),
                       r=[b_x[i], b_rstd, b_AG], w=[b_tmp[s]])
                    op("act", lambda e, kt=kt, s=s: e.activation(out=hT[:, kt, :], in_=tmpf[s][:, :], func=AF.Identity,
                                                                bias=mod_ap(3 * n, kt, j), scale=1.0),
                       r=[b_tmp[s], b_modc], w=[b_hT])

        def ffn_phase(l, f, n):
            wg, wu, wd = wt[(l, f + "_wg")], wt[(l, f + "_wu")], wt[(l, f + "_wd")]
            xTv = xT.rearrange("(kt p) t -> p kt t", p=128)
            with ExitStack() as es2:
                hT = es2.enter_context(sbt("hT", [128, KT, TB], BF16))
                aT = es2.enter_context(sbt("aT", [128, FT, TB], BF16))
                NW = 4
                wb = [es2.enter_context(sbt("wb%d" % i, [128, KT * 128], BF16)) for i in range(NW)]
                b_wb = [cx.buf() for _ in range(NW)]
                xb = [es2.enter_context(sbt("xb%d" % i, [128, TB], F32)) for i in range(4)]
                b_xb = [cx.buf() for _ in range(4)]
                tmpf = [es2.enter_context(sbt("tmpf%d" % i, [128, TB], F32)) for i in range(2)]
                b_tmp = [cx.buf(), cx.buf()]
                sg = [es2.enter_context(sbt("sg%d" % i, [128, TB], F32)) for i in range(2)]
                b_sg = [cx.buf(), cx.buf()]
                b_hT = cx.buf()
                b_aT = [cx.buf() for _ in range(FT)]
                nsc = norm_scratch(es2)
                wi = [0]

                def wload(src_ap, ncols):
                    i = wi[0] % NW
                    wi[0] += 1
                    dma(wb[i][:, 0:ncols], src_ap, w=[b_wb[i]])
                    return wb[i], b_wb[i]

                for blk in range(NBLK):
                    j = 0 if blk * TB < TOKP else 1
                    t0 = blk * TB
                    make_hT(nsc, blk, n, hT, b_hT, xb, b_xb, tmpf, b_tmp)
                    for ft in range(FT):
                        wgt, bg = wload(wg[ft, :, :], KT * 128)
                        wut, bu = wload(wu[ft, :, :], KT * 128)
                        pg, bpg = next_ps()
                        pu, bpu = next_ps()
                        for kt in range(KT):
                            op("pe", lambda e, kt=kt, wgt=wgt, pg=pg: e.matmul(pg[:, 0:TB], wgt[:, kt * 128:(kt + 1) * 128], hT[:, kt, :],
                                                                             start=(kt == 0), stop=(kt == KT - 1)),
                               r=[bg, b_hT], w=[bpg], sig=(kt == KT - 1))
                        for kt in range(KT):
                            op("pe", lambda e, kt=kt, wut=wut, pu=pu: e.matmul(pu[:, 0:TB], wut[:, kt * 128:(kt + 1) * 128], hT[:, kt, :],
                                                                             start=(kt == 0), stop=(kt == KT - 1)),
                               r=[bu, b_hT], w=[bpu], sig=(kt == KT - 1))
                        s = ft % 2
                        op("act", lambda e, s=s, pg=pg: e.activation(out=sg[s][:, :], in_=pg[:, 0:TB], func=AF.Silu),
                           r=[bpg], w=[b_sg[s]])
                        op("dve", lambda e, s=s, pu=pu, ft=ft: e.tensor_tensor(aT[:, ft, :], sg[s][:, :], pu[:, 0:TB], ALU.mult),
                           r=[b_sg[s], bpu], w=[b_aT[ft]])
                    FCH = KT
                    for dt in range(KT):
                        py, bpy = next_ps()
                        i = dt % 4
                        dma(xb[i][:, :], xTv[:, dt, t0:t0 + TB], w=[b_xb[i]])
                        f0 = 0
                        while f0 < FT:
                            nf = min(FCH, FT - f0)
                            wdt_, bd = wload(wd[dt, :, f0 * 128:(f0 + nf) * 128], nf * 128)
                            for q in range(nf):
                                ft = f0 + q
                                op("pe", lambda e, q=q, ft=ft, wdt_=wdt_, py=py: e.matmul(py[:, 0:TB], wdt_[:, q * 128:(q + 1) * 128], aT[:, ft, :],
                                                                                        start=(ft == 0), stop=(ft == FT - 1)),
                                   r=[bd, b_aT[ft]], w=[bpy], sig=(ft == FT - 1))
                            f0 += nf
                        s = dt % 2
                        op("dve", lambda e, i=i, s=s, py=py, dt=dt: e.scalar_tensor_tensor(tmpf[s][:, :], py[:, 0:TB], G_ap(n, dt, j), xb[i][:, :],
                                                                                         ALU.mult, ALU.add),
                           r=[bpy, b_xb[i], b_AG], w=[b_tmp[s]])
                        dma(xTv[:, dt, t0:t0 + TB], tmpf[s][:, :], r=[b_tmp[s]], q="pool")
            cx.barrier()

        def final_phase():
            with ExitStack() as es2:
                hF = es2.enter_context(sbt("hF", [128, KT, TB], F32))
                b_hF = cx.buf()
                xb = [es2.enter_context(sbt("fxb%d" % i, [128, TB], F32)) for i in range(4)]
                b_xb = [cx.buf() for _ in range(4)]
                yo = [es2.enter_context(sbt("yo%d" % i, [128, D], F32)) for i in range(2)]
                b_yo = [cx.buf(), cx.buf()]
                nsc = norm_scratch(es2)
                cnt = 0
                for blk in range(NBLK):
                    make_hT(nsc, blk, 0, hF, b_hF, xb, b_xb, None, None, gain_only=3)
                    for tt in range(TB // 128):
                        i = cnt % 2
                        cnt += 1
                        for k4 in range(KT // 4):
                            ps, bp = next_ps()
                            for q in range(4):
                                kt = k4 * 4 + q
                                op("pe", lambda e, kt=kt, q=q, ps=ps, tt=tt: e.transpose(ps[:, q * 128:(q + 1) * 128],
                                                                                        hF[:, kt, tt * 128:(tt + 1) * 128], ident[:, :]),
                                   r=[b_hF, b_ident], w=[bp], sig=(q == 3))
                            if k4 % 2 == 0:
                                op("act", lambda e, k4=k4, ps=ps, i=i: e.activation(out=yo[i][:, k4 * 512:(k4 + 1) * 512], in_=ps[:, :], func=AF.Copy),
                                   r=[bp], w=[b_yo[i]])
                            else:
                                op("dve", lambda e, k4=k4, ps=ps, i=i: e.tensor_copy(yo[i][:, k4 * 512:(k4 + 1) * 512], ps[:, :]),
                                   r=[bp], w=[b_yo[i]])
                        r0 = blk * TB + tt * 128
                        dma(y_out[r0:r0 + 128, :], yo[i][:, :], r=[b_yo[i]], q="pool")
            cx.barrier()

        AX = mybir.AxisListType.X
        S5W = D // 4
        GW = 3 * D // 8
        HW = D - S5W - GW
        GH = 4
        GK = GW // (2 * GH)
        GV = GW // GH
        HH = HW // 128
        NG = S5W // 16

        def ktiles(K):
            return [(o, min(128, K - o)) for o in range(0, K, 128)]

        wtiles = []
        fm_slots = {}

        def add_fm(key, c0, w):
            fm_slots[key] = len(fm_slots)
            wtiles.append(dict(c0=c0, w=w, fm=True, slot=fm_slots[key]))

        col = 0
        for t in range(S5W // 128):
            add_fm(("u", t), col + t * 128, 128)
        col += S5W
        for nm in ("gq", "gk"):
            for h in range(GH):
                for (o, w) in ktiles(GK):
                    add_fm((nm, h, o), col + h * GK + o, w)
            col += GH * GK
        TM = {}
        tmc = 0
        for nm in ("gv", "gg"):
            TM[nm] = tmc
            for o in range(0, GW, 128):
                wtiles.append(dict(c0=col + o, w=min(128, GW - o), fm=False, tmc=tmc + o))
            tmc += GW
            col += GW
        add_fm(("lr",), col, 32)
        col += 32
        for nm in ("hq", "hzf", "hzb"):
            for h in range(HH):
                add_fm((nm, h), col + h * 128, 128)
            col += HW
        for nm in ("hi", "hg"):
            TM[nm] = tmc
            for o in range(0, HW, 128):
                wtiles.append(dict(c0=col + o, w=128, fm=False, tmc=tmc + o))
            tmc += HW
            col += HW
        assert col == D_IN, (col, D_IN)
        NFM = len(fm_slots)
        NTM = tmc
        projT = dscr("projT", [NFM * 128, TOK], BF16)
        proj_tm = dscr("proj_tm", [TOK, NTM], BF16)
        mixedT = dscr("mixedT", [D, TOK], BF16)
        of_scr = dscr("of_scr", [TOK, max(GV, 128)], F32)
        b_ofscr = cx.buf()
        _ofs_bufs = {}

        def ofs_buf(kind, r0):
            k = (kind, r0)
            if k not in _ofs_bufs:
                _ofs_bufs[k] = cx.buf()
            return _ofs_bufs[k]
        win_t = [dscr("t_win%d" % l, [len(wtiles), 128, KT * 128], BF16) for l in range(DEPTH)]
        wout_t = [dscr("t_wout%d" % l, [KT, 128, KT * 128], BF16) for l in range(DEPTH)]

        identb = cx.sb("identb", [128, 128], BF16)
        op("dve", lambda e: e.tensor_copy(identb[:, :], ident[:, :]), r=[b_ident], w=[b_ident])
        triu = cx.sb("triu", [128, 128], F32)
        tril = cx.sb("tril", [128, 128], F32)
        b_tri = cx.buf()
        dma(triu[:, :], c_triu[:, :], w=[b_tri])
        dma(tril[:, :], c_tril[:, :], w=[b_tri])

        def precast_tiles(src2d, Krows, tiles, dst):
            kts = Krows // 128
            with ExitStack() as es2:
                stg = [es2.enter_context(sbt("pt_s%d" % i, [128, kts * 128], F32)) for i in range(2)]
                outb = [es2.enter_context(sbt("pt_o%d" % i, [128, kts * 128], BF16)) for i in range(2)]
                b_s = [cx.buf(), cx.buf()]
                b_o = [cx.buf(), cx.buf()]
                srcv = src2d.rearrange("(kt p) n -> p kt n", p=128)
                engs = ["act", "dve", "pool"]
                for ct, (c0, w) in enumerate(tiles):
                    i = ct % 2
                    s3 = stg[i][:, :].rearrange("p (kt c) -> p kt c", c=128)
                    o3 = outb[i][:, :].rearrange("p (kt c) -> p kt c", c=128)
                    half = max(1, kts // 2)
                    dma(s3[:, 0:half, 0:w], srcv[:, 0:half, c0:c0 + w], w=[b_s[i]])
                    if half < kts:
                        dma(s3[:, half:kts, 0:w], srcv[:, half:kts, c0:c0 + w], w=[b_s[i]])
                    en = engs[ct % 3]
                    if en == "act":
                        op("act", lambda e, s3=s3, o3=o3, w=w: e.activation(out=o3[:, :, 0:w], in_=s3[:, :, 0:w], func=AF.Copy),
                           r=[b_s[i]], w=[b_o[i]])
                    else:
                        op(en, lambda e, s3=s3, o3=o3, w=w: e.tensor_copy(o3[:, :, 0:w], s3[:, :, 0:w]), r=[b_s[i]], w=[b_o[i]])
                    dv = dst[ct, :, :].rearrange("p (kt c) -> p kt c", c=128)
                    dma(dv[:, :, 0:w], o3[:, :, 0:w], r=[b_o[i]], q="pool")
            cx.barrier()

        if MIX:
            for l in range(DEPTH):
                precast_tiles(fw["w_in"][l], D, [(t["c0"], t["w"]) for t in wtiles], win_t[l])
                precast_tiles(fw["w_out"][l], D, [(c * 128, 128) for c in range(KT)], wout_t[l])

        def win_phase(l):
            with ExitStack() as es2:
                hT = es2.enter_context(sbt("hT", [128, KT, TB], BF16))
                NW = 4
                wb = [es2.enter_context(sbt("wb%d" % i, [128, KT * 128], BF16)) for i in range(NW)]
                b_wb = [cx.buf() for _ in range(NW)]
                xb = [es2.enter_context(sbt("xb%d" % i, [128, TB], F32)) for i in range(4)]
                b_xb = [cx.buf() for _ in range(4)]
                tmpf = [es2.enter_context(sbt("tmpf%d" % i, [128, TB], F32)) for i in range(2)]
                b_tmp = [cx.buf(), cx.buf()]
                st = [es2.enter_context(sbt("wst%d" % i, [128, TB], BF16)) for i in range(3)]
                b_st = [cx.buf() for _ in range(3)]
                b_hT = cx.buf()
                nsc = norm_scratch(es2)
                wi = 0
                si = 0
                NTTB = TB // 128
                for blk in range(NBLK):
                    t0 = blk * TB
                    make_hT(nsc, blk, 1, hT, b_hT, xb, b_xb, tmpf, b_tmp)
                    for ti, tl in enumerate(wtiles):
                        i = wi % NW
                        wi += 1
                        w = tl["w"]
                        dma(wb[i][:, :], win_t[l][ti, :, :], w=[b_wb[i]])
                        ps, bp = next_ps()
                        s = si % 3
                        si += 1
                        if tl["fm"]:
                            for kt in range(KT):
                                op("pe", lambda e, kt=kt, i=i, w=w, ps=ps: e.matmul(ps[0:w, 0:TB], wb[i][:, kt * 128:kt * 128 + w], hT[:, kt, :],
                                                                                   start=(kt == 0), stop=(kt == KT - 1)),
                                   r=[b_wb[i], b_hT], w=[bp], sig=(kt == KT - 1))
                            if s % 2 == 0:
                                op("act", lambda e, s=s, w=w, ps=ps: e.activation(out=st[s][0:w, 0:TB], in_=ps[0:w, 0:TB], func=AF.Copy),
                                   r=[bp], w=[b_st[s]])
                            else:
                                op("dve", lambda e, s=s, w=w, ps=ps: e.tensor_copy(st[s][0:w, 0:TB], ps[0:w, 0:TB]), r=[bp], w=[b_st[s]])
                            r0 = tl["slot"] * 128
                            dma(projT[r0:r0 + w, t0:t0 + TB], st[s][0:w, 0:TB], r=[b_st[s]], q="pool")
                        else:
                            for tt in range(NTTB):
                                for kt in range(KT):
                                    op("pe", lambda e, kt=kt, tt=tt, i=i, w=w, ps=ps: e.matmul(ps[:, tt * 128:tt * 128 + w], hT[:, kt, tt * 128:(tt + 1) * 128],
                                                                                             wb[i][:, kt * 128:kt * 128 + w],
                                                                                             start=(kt == 0), stop=(kt == KT - 1)),
                                       r=[b_wb[i], b_hT], w=[bp], sig=(kt == KT - 1 and tt == NTTB - 1))
                            pv = ps[:, 0:TB].rearrange("p (t c) -> p t c", c=128)
                            sv = st[s][:, 0:TB].rearrange("p (t c) -> p t c", c=128)
                            if s % 2 == 0:
                                op("act", lambda e, pv=pv, sv=sv, w=w: e.activation(out=sv[:, :, 0:w], in_=pv[:, :, 0:w], func=AF.Copy),
                                   r=[bp], w=[b_st[s]])
                            else:
                                op("dve", lambda e, pv=pv, sv=sv, w=w: e.tensor_copy(sv[:, :, 0:w], pv[:, :, 0:w]), r=[bp], w=[b_st[s]])
                            c0 = tl["tmc"]
                            dma(proj_tm[t0:t0 + TB, c0:c0 + w].rearrange("(t p) c -> p t c", p=128), sv[:, :, 0:w], r=[b_st[s]], q="pool")
            cx.barrier()

        def wout_phase(l):
            xTv = xT.rearrange("(kt p) t -> p kt t", p=128)
            mTv = mixedT.rearrange("(kt p) t -> p kt t", p=128)
            with ExitStack() as es2:
                mT = es2.enter_context(sbt("mT", [128, KT, TB], BF16))
                b_mT = cx.buf()
                NW = 4
                wb = [es2.enter_context(sbt("wb%d" % i, [128, KT * 128], BF16)) for i in range(NW)]
                b_wb = [cx.buf() for _ in range(NW)]
                xb = [es2.enter_context(sbt("xb%d" % i, [128, TB], F32)) for i in range(4)]
                b_xb = [cx.buf() for _ in range(4)]
                tmpf = [es2.enter_context(sbt("tmpf%d" % i, [128, TB], F32)) for i in range(2)]
                b_tmp = [cx.buf(), cx.buf()]
                wi = 0
                for blk in range(NBLK):
                    j = 0 if blk * TB < TOKP else 1
                    t0 = blk * TB
                    dma(mT[:, :, :], mTv[:, :, t0:t0 + TB], w=[b_mT])
                    for dt in range(KT):
                        i = wi % NW
                        wi += 1
                        dma(wb[i][:, :], wout_t[l][dt, :, :], w=[b_wb[i]])
                        xi = dt % 4
                        dma(xb[xi][:, :], xTv[:, dt, t0:t0 + TB], w=[b_xb[xi]])
                        py, bpy = next_ps()
                        for kt in range(KT):
                            op("pe", lambda e, kt=kt, i=i, py=py: e.matmul(py[:, 0:TB], wb[i][:, kt * 128:(kt + 1) * 128], mT[:, kt, :],
                                                                         start=(kt == 0), stop=(kt == KT - 1)),
                               r=[b_wb[i], b_mT], w=[bpy], sig=(kt == KT - 1))
                        s = dt % 2
                        op("dve", lambda e, xi=xi, s=s, py=py, dt=dt, j=j: e.scalar_tensor_tensor(tmpf[s][:, :], py[:, 0:TB], G_ap(1, dt, j), xb[xi][:, :],
                                                                                                 ALU.mult, ALU.add),
                           r=[bpy, b_xb[xi], b_AG], w=[b_tmp[s]])
                        dma(xTv[:, dt, t0:t0 + TB], tmpf[s][:, :], r=[b_tmp[s]], q="pool")
            cx.barrier()

        def colize_tiles(es2, row_ap, n, tiles, out_tile, b_out, tag, col0=0):
            rowt = es2.enter_context(sbt("rowt_" + tag, [1, n], F32))
            b_row = cx.buf()
            dma(rowt[0:1, :], row_ap, w=[b_row])
            ps, bp = next_ps()
            op("pe", lambda e: e.matmul(ps[:, 0:len(tiles)], ones_f[0:1, :], ones_f[0:1, 0:len(tiles)], start=True, stop=True),
               r=[b_ones], w=[bp], sig=False)
            for i, (c0, w) in enumerate(tiles):
                op("pe", lambda e, i=i, c0=c0, w=w: e.matmul(ps[0:w, i:i + 1], rowt[0:1, c0:c0 + w], ones_f[0:1, 0:1], start=True, stop=True),
                   r=[b_row, b_ones], w=[bp], sig=(i == len(tiles) - 1))
            op("dve", lambda e: e.tensor_copy(out_tile[:, col0:col0 + len(tiles)], ps[:, 0:len(tiles)]), r=[bp], w=[b_out])

        def lin_attn_phase(l, kind):
            if kind == "gla":
                NH, K, V, C = GH, GK, GV, 128
                SEGL = min(1024, LS)
                qscale = float(GK) ** -0.5
                es_, einv_ = -1.0 / 16.0, 1.0 / 16.0
                mix_off = S5W
                st_in, st_out = st_gla, out_gla
            else:
                NH, K, V, C = HH, 128, 128, 32
                SEGL = LS
                qscale = 128.0 ** -0.5
                es_, einv_ = 1.0, -1.0
                mix_off = S5W + GW
                st_in, st_out = st_hgrn, out_hgrn
            C = min(C, SEQ)
            KTL = ktiles(K)
            VTL = ktiles(V)
            nK = len(KTL)
            TS = max(TOKP, SEGL)
            NCH = TS // C
            colmajor_sample = (kind == "hgrn")
            segs = [dict(t0=0, T=TOKP, nseq=NP, L=SEQ, sample=False, cm=False)]
            for s0 in range(0, LS, SEGL):
                segs.append(dict(t0=TOKP + s0, T=SEGL, nseq=1, L=SEGL, sample=True, cm=colmajor_sample,
                                 first=(s0 == 0), last=(s0 + SEGL >= LS)))
            nlev = int(np.log2(C))
            with ExitStack() as es2:
                qraw = [es2.enter_context(sbt("qraw%d" % i, [128, TS], BF16)) for i in range(nK)]
                kraw = [es2.enter_context(sbt("kraw%d" % i, [128, TS], BF16)) for i in range(nK)]
                kk = [es2.enter_context(sbt("kk%d" % i, [128, TS], BF16)) for i in range(nK)]
                qt = [es2.enter_context(sbt("qt%d" % i, [128, TS], BF16)) for i in range(nK)]
                kt_ = [es2.enter_context(sbt("kt%d" % i, [128, TS], BF16)) for i in range(nK)]
                fA = [es2.enter_context(sbt("fA%d" % i, [128, TS], F32)) for i in range(nK)]
                fB = [es2.enter_context(sbt("fB%d" % i, [128, TS], F32)) for i in range(nK)]
                b_raw, b_kk, b_qt, b_kt, b_fA, b_fB = (cx.buf() for _ in range(6))
                lrT = es2.enter_context(sbt("lrT", [32, TS], BF16))
                b_lrT = cx.buf()
                vslab = es2.enter_context(sbt("vslab", [C, NCH * V], BF16))
                gslab = es2.enter_context(sbt("gslab", [C, NCH * V], BF16))
                b_vs, b_gs = cx.buf(), cx.buf()
                outT = [es2.enter_context(sbt("outT%d" % i, [128, TS], BF16)) for i in range(len(VTL))]
                b_outT = cx.buf()
                S = [es2.enter_context(sbt("S%d" % i, [128, V], F32)) for i in range(nK)]
                Sbf = [es2.enter_context(sbt("Sbf%d" % i, [128, V], BF16)) for i in range(nK)]
                b_S, b_Sbf = cx.buf(), cx.buf()
                amb = [es2.enter_context(sbt("am%d" % i, [C, C], BF16)) for i in range(3)]
                b_am = [cx.buf() for _ in range(3)]
                ktm = [es2.enter_context(sbt("ktm%d" % i, [C, K], BF16)) for i in range(3)]
                b_ktm = [cx.buf() for _ in range(3)]
                ofb = [es2.enter_context(sbt("ofb%d" % i, [C, V], F32)) for i in range(4)]
                b_of = [cx.buf() for _ in range(4)]
                osb = [es2.enter_context(sbt("osb%d" % i, [C, V], F32)) for i in range(2)]
                sqb = [es2.enter_context(sbt("sqb%d" % i, [C, V], F32)) for i in range(2)]
                gsb = [es2.enter_context(sbt("gsb%d" % i, [C, V], F32)) for i in range(2)]
                resb = [es2.enter_context(sbt("resb%d" % i, [C, V], BF16)) for i in range(2)]
                ssb = [es2.enter_context(sbt("ssb%d" % i, [C, 2], F32)) for i in range(2)]
                b_fin = [cx.buf(), cx.buf()]
                gnb = es2.enter_context(sbt("gnb", [128, V], F32))
                b_gnb = cx.buf()
                grow = es2.enter_context(sbt("grow", [1, V], F32))
                b_grow = cx.buf()
                gsrc = (gla_norm_g if kind == "gla" else hg_norm_g)
                dma(grow[0:1, :], gsrc[l:l + 1, :], w=[b_grow])
                ps, bp = next_ps()
                op("pe", lambda e: e.matmul(ps[:, 0:V], ones_f[0:1, :], grow[0:1, :], start=True, stop=True), r=[b_grow, b_ones], w=[bp])
                op("dve", lambda e: e.tensor_copy(gnb[:, :], ps[:, 0:V]), r=[bp], w=[b_gnb])
                if kind == "gla":
                    w2f = es2.enter_context(sbt("w2f", [32, 2 * GH * GK], F32))
                    w2p = es2.enter_context(sbt("w2p", [32, 2 * GH * GK], BF16))
                    b_w2 = cx.buf()
                    op("pool", lambda e: e.memset(w2f[:, :], 0.0), w=[b_w2])
                    for d in range(2):
                        dma(w2f[d * 16:(d + 1) * 16, d * GH * GK:(d + 1) * GH * GK], gla_w2[l, d, :, :], w=[b_w2])
                    op("dve", lambda e: e.tensor_copy(w2p[:, :], w2f[:, :]), r=[b_w2], w=[b_w2])
                    nb2 = es2.enter_context(sbt("nb2", [128, 2 * GH * nK], F32))
                    b_nb2 = cx.buf()
                    for d in range(2):
                        tl = [(h * GK + o, w) for h in range(GH) for (o, w) in KTL]
                        colize_tiles(es2, gla_b2[l, d:d + 1, :], GH * GK, tl, nb2, b_nb2, "b2%d" % d, col0=d * GH * nK)
                    op("dve", lambda e: e.tensor_scalar(nb2[:, :], nb2[:, :], -1.0, None, ALU.mult), r=[b_nb2], w=[b_nb2])
                else:
                    lbc = es2.enter_context(sbt("lbc", [128, 4 * HH], F32))
                    rw = es2.enter_context(sbt("rw", [128, 4 * HH], F32))
                    b_lbc = cx.buf()
                    b_rw = cx.buf()
                    if l == 0:
                        op("pool", lambda e: e.memset(lbc[:, 0:2 * HH], 0.0), w=[b_lbc])
                        op("pool", lambda e: e.memset(lbc[:, 2 * HH:4 * HH], 1.0), w=[b_lbc])
                    else:
                        for d in range(2):
                            for ll in range(2):
                                colize(es2, hg_lb_raw[d, ll:ll + 1, :], HW, rw, (d * 2 + ll) * HH, b_rw, "lb%d%d" % (d, ll))
                        for d in range(2):
                            op("dve", lambda e, d=d: e.tensor_tensor(lbc[:, d * HH:(d + 1) * HH], rw[:, (d * 2 + 1) * HH:(d * 2 + 2) * HH],
                                                                    rw[:, (d * 2) * HH:(d * 2 + 1) * HH], ALU.subtract), r=[b_rw], w=[b_lbc])
                        op("act", lambda e: e.activation(out=lbc[:, 0:2 * HH], in_=lbc[:, 0:2 * HH], func=AF.Sigmoid), r=[b_lbc], w=[b_lbc])
                        op("dve", lambda e: e.tensor_scalar(lbc[:, 2 * HH:4 * HH], lbc[:, 0:2 * HH], -1.0, 1.0, ALU.mult, ALU.add), r=[b_lbc], w=[b_lbc])

                def chunk_list(seg, d):
                    out = []
                    if seg["cm"]:
                        R = seg["T"] // 64
                        HC = R // C
                        order = [(c, hh) for c in range(64) for hh in range(HC)]
                        if d == 1:
                            order = order[::-1]
                        for idx, (c, hh) in enumerate(order):
                            def sel(t, c=c, hh=hh):
                                return t.rearrange("p (r c) -> p c r", c=64)[:, c, hh * C:(hh + 1) * C]
                            out.append(dict(sel=sel, n=c * HC + hh, first=(idx == 0), last=(idx == len(order) - 1), seq=0))
                    else:
                        ncs = seg["L"] // C
                        for sq_ in range(seg["nseq"]):
                            rng_ = list(range(ncs))
                            if d == 1:
                                rng_ = rng_[::-1]
                            for idx, cc in enumerate(rng_):
                                n = sq_ * ncs + cc
                                def sel(t, n=n):
                                    return t[:, n * C:(n + 1) * C]
                                fl = (idx == 0)
                                la = (idx == ncs - 1)
                                if seg["sample"]:
                                    fl = fl and (seg["first"] if d == 0 else seg["last"])
                                    la = la and (seg["last"] if d == 0 else seg["first"])
                                out.append(dict(sel=sel, n=n, first=fl, last=la, seq=sq_))
                    return out

                def cview(t, seg):
                    T = seg["T"]
                    if seg["cm"]:
                        return t[:, 0:T].rearrange("p (h i c) -> p h c i", i=C, c=64)
                    return t[:, 0:T].rearrange("p (n c) -> p n c", c=C)

                def sl(v, lo, hi):
                    if len(v.shape) == 4:
                        return v[:, :, :, lo:hi]
                    return v[:, :, lo:hi]

                cnt = [0]
                for h in range(NH):
                    for d in range(2):
                        seg_order = segs if d == 0 else segs[::-1]
                        for seg in seg_order:
                            T, t0 = seg["T"], seg["t0"]
                            nch = T // C
                            for ki, (ko, kw) in enumerate(KTL):
                                if kind == "gla":
                                    rq = fm_slots[("gq", h, ko)] * 128
                                    rk = fm_slots[("gk", h, ko)] * 128
                                else:
                                    rq = fm_slots[("hq", h)] * 128
                                    rk = fm_slots[("hzf" if d == 0 else "hzb", h)] * 128
                                dma(qraw[ki][0:kw, 0:T], projT[rq:rq + kw, t0:t0 + T], w=[b_raw])
                                dma(kraw[ki][0:kw, 0:T], projT[rk:rk + kw, t0:t0 + T], w=[b_raw])
                            vnm, gnm = ("gv", "gg") if kind == "gla" else ("hi", "hg")
                            for (slab, nm, bb) in ((vslab, vnm, b_vs), (gslab, gnm, b_gs)):
                                c0 = TM[nm] + h * V
                                if seg["cm"]:
                                    HC = (T // 64) // C
                                    dv = slab[:, 0:nch * V].rearrange("p (c h v) -> p c h v", h=HC, v=V)
                                    sv = proj_tm[t0:t0 + T, c0:c0 + V].rearrange("(h i c) v -> h i c v", i=C, c=64)
                                    for hh in range(HC):
                                        dma(dv[:, :, hh, :], sv[hh], w=[bb])
                                else:
                                    dma(slab[:, 0:nch * V].rearrange("p (n v) -> p n v", v=V),
                                        proj_tm[t0:t0 + T, c0:c0 + V].rearrange("(n i) v -> i n v", i=C), w=[bb])
                            if kind == "gla":
                                rl = fm_slots[("lr",)] * 128
                                dma(lrT[:, 0:T], projT[rl:rl + 32, t0:t0 + T], w=[b_lrT])
                                for ki, (ko, kw) in enumerate(KTL):
                                    cw = d * GH * GK + h * GK + ko
                                    bcol = nb2[0:kw, (d * GH + h) * nK + ki:(d * GH + h) * nK + ki + 1]
                                    for c5 in range(0, T, 512):
                                        n5 = min(512, T - c5)
                                        ps, bp = next_ps()
                                        op("pe", lambda e, ps=ps, cw=cw, kw=kw, c5=c5, n5=n5: e.matmul(ps[0:kw, 0:n5], w2p[:, cw:cw + kw], lrT[:, c5:c5 + n5],
                                                                                                     start=True, stop=True),
                                           r=[b_w2, b_lrT], w=[bp])
                                        op("act", lambda e, ps=ps, ki=ki, kw=kw, c5=c5, n5=n5, bcol=bcol: e.activation(
                                            out=fA[ki][0:kw, c5:c5 + n5], in_=ps[0:kw, 0:n5], func=AF.Exp, bias=bcol, scale=-1.0),
                                           r=[bp, b_nb2], w=[b_fA])
                                    op("act", lambda e, ki=ki, kw=kw: e.activation(out=fA[ki][0:kw, 0:T], in_=fA[ki][0:kw, 0:T], func=AF.Ln, bias=1.0, scale=1.0),
                                       r=[b_fA], w=[b_fA])
                                    ksrc = kraw
                            else:
                                ki, kw = 0, 128
                                lb_ap = lbc[:, d * HH + h:d * HH + h + 1]
                                oml_ap = lbc[:, 2 * HH + d * HH + h:2 * HH + d * HH + h + 1]
                                op("act", lambda e: e.activation(out=fB[0][:, 0:T], in_=kraw[0][:, 0:T], func=AF.Sigmoid), r=[b_raw], w=[b_fB])
                                op("dve", lambda e, lb_ap=lb_ap, oml_ap=oml_ap: e.tensor_scalar(fB[0][:, 0:T], fB[0][:, 0:T], oml_ap, lb_ap, ALU.mult, ALU.add),
                                   r=[b_fB, b_lbc], w=[b_fB])
                                op("pool", lambda e: e.tensor_scalar(kk[0][:, 0:T], fB[0][:, 0:T], -1.0, 1.0, ALU.mult, ALU.add), r=[b_fB], w=[b_kk])
                                op("act", lambda e: e.activation(out=fA[0][:, 0:T], in_=fB[0][:, 0:T], func=AF.Ln), r=[b_fB], w=[b_fA])
                                ksrc = kk
                            res_in_A = True
                            for lev in range(nlev):
                                dd = 1 << lev
                                for ki, (ko, kw) in enumerate(KTL):
                                    src, dst = (fA, fB) if res_in_A else (fB, fA)
                                    sv_ = cview(src[ki][0:kw, :], seg)
                                    dv_ = cview(dst[ki][0:kw, :], seg)
                                    if d == 0:
                                        op("dve", lambda e, sv_=sv_, dv_=dv_, dd=dd: e.tensor_tensor(sl(dv_, dd, C), sl(sv_, dd, C), sl(sv_, 0, C - dd), ALU.add),
                                           r=[b_fA, b_fB], w=[b_fA, b_fB])
                                        op("pool", lambda e, sv_=sv_, dv_=dv_, dd=dd: e.tensor_copy(sl(dv_, 0, dd), sl(sv_, 0, dd)),
                                           r=[b_fA, b_fB], w=[b_fA, b_fB])
                                    else:
                                        op("dve", lambda e, sv_=sv_, dv_=dv_, dd=dd: e.tensor_tensor(sl(dv_, 0, C - dd), sl(sv_, 0, C - dd), sl(sv_, dd, C), ALU.add),
                                           r=[b_fA, b_fB], w=[b_fA, b_fB])
                                        op("pool", lambda e, sv_=sv_, dv_=dv_, dd=dd: e.tensor_copy(sl(dv_, C - dd, C), sl(sv_, C - dd, C)),
                                           r=[b_fA, b_fB], w=[b_fA, b_fB])
                                res_in_A = not res_in_A
                            X, Y = (fA, fB) if res_in_A else (fB, fA)
                            for ki, (ko, kw) in enumerate(KTL):
                                op("act", lambda e, ki=ki, kw=kw: e.activation(out=Y[ki][0:kw, 0:T], in_=X[ki][0:kw, 0:T], func=AF.Exp, scale=es_),
                                   r=[b_fA, b_fB], w=[b_fA, b_fB])
                                op("act", lambda e, ki=ki, kw=kw: e.activation(out=X[ki][0:kw, 0:T], in_=X[ki][0:kw, 0:T], func=AF.Exp, scale=einv_),
                                   r=[b_fA, b_fB], w=[b_fA, b_fB])
                                op("dve", lambda e, ki=ki, kw=kw: e.scalar_tensor_tensor(qt[ki][0:kw, 0:T], qraw[ki][0:kw, 0:T], qscale, Y[ki][0:kw, 0:T],
                                                                                       ALU.mult, ALU.mult),
                                   r=[b_raw, b_fA, b_fB], w=[b_qt])
                                op("pool", lambda e, ki=ki, kw=kw, ksrc=ksrc: e.tensor_tensor(kt_[ki][0:kw, 0:T], ksrc[ki][0:kw, 0:T], X[ki][0:kw, 0:T], ALU.mult),
                                   r=[b_raw, b_kk, b_fA, b_fB], w=[b_kt])
                            E = Y
                            mask = triu if d == 0 else tril
                            for ch in chunk_list(seg, d):
                                ci = cnt[0]
                                cnt[0] += 1
                                sel, n = ch["sel"], ch["n"]
                                if ch["first"]:
                                    for ki, (ko, kw) in enumerate(KTL):
                                        if seg["sample"]:
                                            dma(S[ki][0:kw, :], st_in[l, d, h, ko:ko + kw, :], w=[b_S])
                                        else:
                                            op("pool", lambda e, ki=ki: e.memset(S[ki][:, :], 0.0), w=[b_S])
                                        op("act", lambda e, ki=ki, kw=kw: e.activation(out=Sbf[ki][0:kw, :], in_=S[ki][0:kw, :], func=AF.Copy),
                                           r=[b_S], w=[b_Sbf])
                                vch = vslab[:, n * V:(n + 1) * V]
                                pa, bpa = next_ps()
                                for ki, (ko, kw) in enumerate(KTL):
                                    op("pe", lambda e, ki=ki, kw=kw, pa=pa: e.matmul(pa[0:C, 0:C], sel(kt_[ki][0:kw, :]), sel(qt[ki][0:kw, :]),
                                                                                   start=(ki == 0), stop=(ki == nK - 1)),
                                       r=[b_kt, b_qt], w=[bpa], sig=(ki == nK - 1))
                                a3 = ci % 3
                                op("dve", lambda e, a3=a3, pa=pa: e.tensor_tensor(amb[a3][:, :], pa[0:C, 0:C], mask[0:C, 0:C], ALU.mult),
                                   r=[bpa, b_tri], w=[b_am[a3]])
                                po, bpo = next_ps()
                                op("pe", lambda e, a3=a3, po=po, vch=vch: e.matmul(po[0:C, 0:V], amb[a3][:, :], vch, start=True, stop=False),
                                   r=[b_am[a3], b_vs], w=[bpo], sig=False)
                                for ki, (ko, kw) in enumerate(KTL):
                                    op("pe", lambda e, ki=ki, kw=kw, po=po: e.matmul(po[0:C, 0:V], sel(qt[ki][0:kw, :]), Sbf[ki][0:kw, :],
                                                                                   start=False, stop=(ki == nK - 1)),
                                       r=[b_qt, b_Sbf], w=[bpo], sig=(ki == nK - 1))
                                pk, bpk = next_ps()
                                for ki, (ko, kw) in enumerate(KTL):
                                    op("pe", lambda e, ki=ki, ko=ko, kw=kw, pk=pk: e.matmul(pk[0:C, ko:ko + kw], sel(kt_[ki][0:kw, :]), identb[0:kw, 0:kw],
                                                                                          start=True, stop=True),
                                       r=[b_kt, b_ident], w=[bpk], sig=(ki == nK - 1))
                                op("act", lambda e, a3=a3, pk=pk: e.activation(out=ktm[a3][:, :], in_=pk[0:C, 0:K], func=AF.Copy), r=[bpk], w=[b_ktm[a3]])
                                for ki, (ko, kw) in enumerate(KTL):
                                    pS, bpS = next_ps()
                                    op("pe", lambda e, a3=a3, ko=ko, kw=kw, pS=pS, vch=vch: e.matmul(pS[0:kw, 0:V], ktm[a3][:, ko:ko + kw], vch, start=True, stop=True),
                                       r=[b_ktm[a3], b_vs], w=[bpS])
                                    op("dve", lambda e, ki=ki, kw=kw, pS=pS: e.tensor_tensor(S[ki][0:kw, :], pS[0:kw, 0:V], S[ki][0:kw, :], ALU.add),
                                       r=[bpS, b_S], w=[b_S])
                                    ecol = sel(E[ki][0:kw, :])
                                    dec = ecol[:, C - 1:C] if d == 0 else ecol[:, 0:1]
                                    op("pool", lambda e, ki=ki, kw=kw, dec=dec: e.tensor_scalar(S[ki][0:kw, :], S[ki][0:kw, :], dec, None, ALU.mult),
                                       r=[b_S, b_fA, b_fB], w=[b_S])
                                    op("act", lambda e, ki=ki, kw=kw: e.activation(out=Sbf[ki][0:kw, :], in_=S[ki][0:kw, :], func=AF.Copy),
                                       r=[b_S], w=[b_Sbf])
                                    if ch["last"] and not seg["sample"]:
                                        dma(st_out[ch["seq"], l, d, h, ko:ko + kw, :], S[ki][0:kw, :], r=[b_S], q="pool")
                                r0 = t0 + n * C
                                o4 = ci % 4
                                if d == 0:
                                    op("act", lambda e, o4=o4, po=po: e.activation(out=ofb[o4][:, :], in_=po[0:C, 0:V], func=AF.Copy), r=[bpo], w=[b_of[o4]])
                                    dma(of_scr[r0:r0 + C, 0:V], ofb[o4][:, :], r=[b_of[o4]], w=[ofs_buf(kind, r0)], q="pool")
                                else:
                                    f2 = ci % 2
                                    dma(ofb[o4][:, :], of_scr[r0:r0 + C, 0:V], r=[ofs_buf(kind, r0)], w=[b_of[o4]])
                                    op("dve", lambda e, o4=o4, f2=f2, po=po: e.tensor_tensor(osb[f2][:, :], po[0:C, 0:V], ofb[o4][:, :], ALU.add),
                                       r=[bpo, b_of[o4]], w=[b_fin[f2]])
                                    op("pool", lambda e, f2=f2: e.tensor_tensor(sqb[f2][:, :], osb[f2][:, :], osb[f2][:, :], ALU.mult), r=[b_fin[f2]], w=[b_fin[f2]])
                                    op("dve", lambda e, f2=f2: e.tensor_reduce(ssb[f2][:, 0:1], sqb[f2][:, :], AX, ALU.add), r=[b_fin[f2]], w=[b_fin[f2]])
                                    op("dve", lambda e, f2=f2: e.tensor_scalar(ssb[f2][:, 0:1], ssb[f2][:, 0:1], 1.0 / V, 1e-6, ALU.mult, ALU.add),
                                       r=[b_fin[f2]], w=[b_fin[f2]])
                                    op("act", lambda e, f2=f2: e.activation(out=ssb[f2][:, 0:1], in_=ssb[f2][:, 0:1], func=AF.Sqrt), r=[b_fin[f2]], w=[b_fin[f2]])
                                    op("dve", lambda e, f2=f2: e.reciprocal(ssb[f2][:, 0:1], ssb[f2][:, 0:1]), r=[b_fin[f2]], w=[b_fin[f2]])
                                    op("dve", lambda e, f2=f2: e.scalar_tensor_tensor(osb[f2][:, :], osb[f2][:, :], ssb[f2][:, 0:1], gnb[0:C, :], ALU.mult, ALU.mult),
                                       r=[b_fin[f2], b_gnb], w=[b_fin[f2]])
                                    gch = gslab[:, n * V:(n + 1) * V]
                                    op("act", lambda e, f2=f2, gch=gch: e.activation(out=gsb[f2][:, :], in_=gch, func=AF.Silu), r=[b_gs], w=[b_fin[f2]])
                                    op("dve", lambda e, f2=f2: e.tensor_tensor(resb[f2][:, :], osb[f2][:, :], gsb[f2][:, :], ALU.mult), r=[b_fin[f2]], w=[b_fin[f2]])
                                    for vi, (vo, vw) in enumerate(VTL):
                                        pt, bpt = next_ps()
                                        op("pe", lambda e, f2=f2, vo=vo, vw=vw, pt=pt: e.matmul(pt[0:vw, 0:C], resb[f2][:, vo:vo + vw], identb[0:C, 0:C], start=True, stop=True),
                                           r=[b_fin[f2], b_ident], w=[bpt])
                                        op("act", lambda e, vi=vi, vw=vw, pt=pt: e.activation(out=sel(outT[vi][0:vw, :]), in_=pt[0:vw, 0:C], func=AF.Copy),
                                           r=[bpt], w=[b_outT])
                            if d == 1:
                                for vi, (vo, vw) in enumerate(VTL):
                                    r0 = mix_off + h * V + vo
                                    dma(mixedT[r0:r0 + vw, t0:t0 + T], outT[vi][0:vw, 0:T], r=[b_outT], q="pool")
            cx.barrier()

        NT5 = NG // 2
        NU = S5W // 128
        s5scr = dscr("s5scr", [8, NG * 64], F32)
        yaT = dscr("yaT", [S5W, TOK], BF16)
        glu_t = [dscr("t_glu%d" % l, [NU, 128, S5W], BF16) for l in range(DEPTH)]
        if MIX:
            for l in range(DEPTH):
                precast_tiles(s5_glu_w[l], S5W, [(c * 128, 128) for c in range(NU)], glu_t[l])
        PI = float(np.pi)

        def s5_phase(l):
            TS = max(TOKP, LS)
            LMAX = max(SEQ, LS)
            NLEV = int(np.log2(LMAX))
            with ExitStack() as es2:
                NV = 7 + 3 * NLEV
                pc = [es2.enter_context(sbt("s5pc%d" % d, [128, NV * NT5], F32)) for d in range(2)]
                b_pc = [cx.buf(), cx.buf()]

                def pcol(d, v, t):
                    return pc[d][:, v * NT5 + t:v * NT5 + t + 1]

                def pall(d, v):
                    return pc[d][:, v * NT5:(v + 1) * NT5]
                dcol = es2.enter_context(sbt("s5dcol", [128, NU], F32))
                gbcol = es2.enter_context(sbt("s5gb", [128, NU], F32))
                b_dcol = cx.buf()
                colize(es2, s5_d[l:l + 1, :], S5W, dcol, 0, b_dcol, "s5d")
                colize(es2, s5_glu_b[l:l + 1, :], S5W, gbcol, 0, b_dcol, "s5gb")
                nat = {nm: es2.enter_context(sbt("s5n_" + nm, [NG, 64], F32)) for nm in
                       ("are", "aim", "mag", "ang", "sn", "cs", "lre", "lim", "den", "t1", "t2", "zre", "zim")}
                dtc = es2.enter_context(sbt("s5dt", [NG, 1], F32))
                hpi = es2.enter_context(sbt("s5hpi", [128, 1], F32))
                b_hpi = cx.buf()
                op("pool", lambda e: e.memset(hpi[:, :], PI / 2), w=[b_hpi])
                b_nat = cx.buf()
                b_scr = cx.buf()
                for d in range(2):
                    dma(nat["are"][:, :], s5_a_re[l, d, :, :], w=[b_nat])
                    dma(nat["aim"][:, :], s5_a_im[l, d, :, :], w=[b_nat])
                    dma(dtc[:, :], s5_log_dt[l, d, :].rearrange("(g o) -> g o", o=1), w=[b_nat])
                    N = nat

                    def o_(en, fn):
                        op(en, fn, r=[b_nat, b_hpi], w=[b_nat])
                    o_("act", lambda e: e.activation(out=dtc[:, :], in_=dtc[:, :], func=AF.Exp))
                    o_("act", lambda e: e.activation(out=N["mag"][:, :], in_=N["are"][:, :], func=AF.Exp, scale=dtc[:, 0:1]))
                    o_("dve", lambda e: e.tensor_scalar(N["ang"][:, :], N["aim"][:, :], dtc[:, 0:1], None, ALU.mult))
                    o_("act", lambda e: e.activation(out=N["sn"][:, :], in_=N["ang"][:, :], func=AF.Sin, scale=0.125))
                    o_("act", lambda e: e.activation(out=N["cs"][:, :], in_=N["ang"][:, :], func=AF.Sin, scale=-0.125, bias=hpi[0:NG, 0:1]))
                    for _ in range(3):
                        o_("dve", lambda e: e.tensor_tensor(N["t1"][:, :], N["cs"][:, :], N["cs"][:, :], ALU.mult))
                        o_("dve", lambda e: e.tensor_tensor(N["t2"][:, :], N["sn"][:, :], N["sn"][:, :], ALU.mult))
                        o_("dve", lambda e: e.scalar_tensor_tensor(N["sn"][:, :], N["sn"][:, :], 2.0, N["cs"][:, :], ALU.mult, ALU.mult))
                        o_("dve", lambda e: e.tensor_tensor(N["cs"][:, :], N["t1"][:, :], N["t2"][:, :], ALU.subtract))
                    o_("dve", lambda e: e.tensor_tensor(N["lre"][:, :], N["mag"][:, :], N["cs"][:, :], ALU.mult))
                    o_("dve", lambda e: e.tensor_tensor(N["lim"][:, :], N["mag"][:, :], N["sn"][:, :], ALU.mult))
                    o_("dve", lambda e: e.tensor_tensor(N["den"][:, :], N["are"][:, :], N["are"][:, :], ALU.mult))
                    o_("dve", lambda e: e.tensor_tensor(N["t1"][:, :], N["aim"][:, :], N["aim"][:, :], ALU.mult))
                    o_("dve", lambda e: e.tensor_tensor(N["den"][:, :], N["den"][:, :], N["t1"][:, :], ALU.add))
                    o_("dve", lambda e: e.reciprocal(N["den"][:, :], N["den"][:, :]))
                    o_("dve", lambda e: e.tensor_scalar(N["t1"][:, :], N["lre"][:, :], -1.0, None, ALU.add))
                    o_("dve", lambda e: e.tensor_tensor(N["zre"][:, :], N["t1"][:, :], N["are"][:, :], ALU.mult))
                    o_("dve", lambda e: e.tensor_tensor(N["t2"][:, :], N["lim"][:, :], N["aim"][:, :], ALU.mult))
                    o_("dve", lambda e: e.tensor_tensor(N["zre"][:, :], N["zre"][:, :], N["t2"][:, :], ALU.add))
                    o_("dve", lambda e: e.tensor_tensor(N["zre"][:, :], N["zre"][:, :], N["den"][:, :], ALU.mult))
                    o_("dve", lambda e: e.tensor_tensor(N["zim"][:, :], N["lim"][:, :], N["are"][:, :], ALU.mult))
                    o_("dve", lambda e: e.tensor_tensor(N["t2"][:, :], N["t1"][:, :], N["aim"][:, :], ALU.mult))
                    o_("dve", lambda e: e.tensor_tensor(N["zim"][:, :], N["zim"][:, :], N["t2"][:, :], ALU.subtract))
                    o_("dve", lambda e: e.tensor_tensor(N["zim"][:, :], N["zim"][:, :], N["den"][:, :], ALU.mult))
                    for vi, nm in enumerate(("lre", "lim", "zre", "zim")):
                        row = d * 4 + vi
                        dma(s5scr[row, :].rearrange("(g p) -> g p", p=64), N[nm][:, :], r=[b_nat], w=[b_scr], q="pool")
                        colize(es2, s5scr[row:row + 1, :], NG * 64, pc[d], vi * NT5, b_pc[d], "s5c%d%d" % (d, vi), rd=[b_scr])
                    colize(es2, st_s5re[l, d:d + 1, :], NG * 64, pc[d], 5 * NT5, b_pc[d], "s5h%dr" % d)
                    colize(es2, st_s5im[l, d:d + 1, :], NG * 64, pc[d], 6 * NT5, b_pc[d], "s5h%di" % d)
                    tmpc = es2.enter_context(sbt("s5tc", [128, 4 * NT5], F32))

                    def p_(en, fn):
                        op(en, fn, r=[b_pc[d]], w=[b_pc[d]])
                    T0, T1, T2, T3 = (tmpc[:, i * NT5:(i + 1) * NT5] for i in range(4))
                    p_("dve", lambda e: e.tensor_tensor(T0, pall(d, 0), pall(d, 5), ALU.mult))
                    p_("dve", lambda e: e.tensor_tensor(T1, pall(d, 1), pall(d, 6), ALU.mult))
                    p_("dve", lambda e: e.tensor_tensor(T2, pall(d, 0), pall(d, 6), ALU.mult))
                    p_("dve", lambda e: e.tensor_tensor(T3, pall(d, 1), pall(d, 5), ALU.mult))
                    p_("dve", lambda e: e.tensor_tensor(pall(d, 5), T0, T1, ALU.subtract))
                    p_("dve", lambda e: e.tensor_tensor(pall(d, 6), T2, T3, ALU.add))
                    p_("dve", lambda e: e.tensor_scalar(pall(d, 4), pall(d, 3), -1.0, None, ALU.mult))
                    p_("dve", lambda e: e.tensor_copy(pall(d, 7), pall(d, 0)))
                    p_("dve", lambda e: e.tensor_copy(pall(d, 8), pall(d, 1)))
                    for k in range(NLEV):
                        v = 7 + 3 * k
                        p_("dve", lambda e, v=v: e.tensor_scalar(pall(d, v + 2), pall(d, v + 1), -1.0, None, ALU.mult))
                        if k + 1 < NLEV:
                            p_("dve", lambda e, v=v: e.tensor_tensor(T0, pall(d, v), pall(d, v), ALU.mult))
                            p_("dve", lambda e, v=v: e.tensor_tensor(T1, pall(d, v + 1), pall(d, v + 1), ALU.mult))
                            p_("dve", lambda e, v=v: e.tensor_tensor(pall(d, v + 3), T0, T1, ALU.subtract))
                            p_("dve", lambda e, v=v: e.tensor_tensor(T2, pall(d, v), pall(d, v + 1), ALU.mult))
                            p_("dve", lambda e, v=v: e.tensor_scalar(pall(d, v + 4), T2, 2.0, None, ALU.mult))
                Ar = es2.enter_context(sbt("s5Ar", [128, TS], F32))
                Ai = es2.enter_context(sbt("s5Ai", [128, TS], F32))
                Br = es2.enter_context(sbt("s5Br", [128, TS], F32))
                Bi = es2.enter_context(sbt("s5Bi", [128, TS], F32))
                Xr = es2.enter_context(sbt("s5Xr", [128, TS], BF16))
                Xi = es2.enter_context(sbt("s5Xi", [128, TS], BF16))
                b_re, b_im, b_X = cx.buf(), cx.buf(), cx.buf()
                uT = es2.enter_context(sbt("s5uT", [128, TS], BF16))
                b_uT = cx.buf()
                yacc = es2.enter_context(sbt("s5yacc", [128, TS], F32))
                b_yacc = cx.buf()
                yab = es2.enter_context(sbt("s5yab", [128, TS], BF16))
                b_yab = cx.buf()
                fin = es2.enter_context(sbt("s5fin", [128, 2 * NP], F32))
                b_fin5 = cx.buf()
                lst = [es2.enter_context(sbt("s5ls%d" % i, [128, 128], F32)) for i in range(2)]
                b_lst = [cx.buf(), cx.buf()]
                LW = {}
                for nm in ("br", "bi", "cr", "ci"):
                    LW[nm] = [es2.enter_context(sbt("s5L%s%d" % (nm, i), [128, 128], BF16)) for i in range(2)]
                b_LW = {nm: [cx.buf(), cx.buf()] for nm in LW}
                lcnt = [0]
                wcnt = [0]

                def build_lhsT(kind, d, t, slot):
                    i = lcnt[0] % 2
                    lcnt[0] += 1
                    op("pool", lambda e, i=i: e.memset(lst[i][:, :], 0.0), w=[b_lst[i]])
                    for e_ in range(2):
                        g = 2 * t + e_
                        gl = g % 8
                        if kind in ("br", "bi"):
                            src = (s5_b_re if kind == "br" else s5_b_im)[l, d, g, :, :].rearrange("p c -> c p")
                            dst = lst[i][gl * 16:(gl + 1) * 16, e_ * 64:(e_ + 1) * 64]
                        else:
                            src = (s5_c_re if kind == "cr" else s5_c_im)[l, d, g, :, :].rearrange("c p -> p c")
                            dst = lst[i][e_ * 64:(e_ + 1) * 64, gl * 16:(gl + 1) * 16]
                        E_ = cx.E["sp"]
                        slot_ = E_.pool[E_.ndma % len(E_.pool)]
                        E_.ndma += 1
                        deps = cx._deps([], [b_lst[i]])
                        if slot_[1] > 0:
                            deps.append((slot_[0], slot_[1]))
                        cx._wait(E_, deps)
                        slot_[1] += 16
                        E_.eng.dma_start(out=dst, in_=src, allow_slow_non_contiguous=True).then_inc(slot_[0], 16)
                        k_ = id(slot_[0])
                        b_lst[i].w[k_] = (slot_[0], slot_[1])
                    sc = -1.0 if kind == "ci" else 1.0
                    op("dve", lambda e, i=i, slot=slot, sc=sc: e.tensor_scalar(LW[kind][slot][:, :], lst[i][:, :], sc, None, ALU.mult),
                       r=[b_lst[i]], w=[b_LW[kind][slot]])

                seqsets = [dict(t0=0, nseq=NP, L=SEQ, sample=False), dict(t0=TOKP, nseq=1, L=LS, sample=True)]
                S5DBG = cfg.get("S5DBG", ())
                if "noprompt" in S5DBG:
                    seqsets = seqsets[1:]
                if "nosample" in S5DBG:
                    seqsets = seqsets[:1]
                for ss_ in seqsets:
                    t0, nseq, L = ss_["t0"], ss_["nseq"], ss_["L"]
                    T = nseq * L
                    nlev = int(np.log2(L))

                    def v3(t_, lo, hi):
                        return t_[:, 0:T].rearrange("p (s q) -> p s q", q=L)[:, :, lo:hi]
                    for j8 in range(NU):
                        r0 = fm_slots[("u", j8)] * 128
                        dma(uT[:, 0:T], projT[r0:r0 + 128, t0:t0 + T], w=[b_uT])
                        first_acc = True
                        for tl in range(4):
                            t = j8 * 4 + tl
                            for d in range(2):
                                slot = wcnt[0] % 2
                                wcnt[0] += 1
                                for kind in ("br", "bi", "cr", "ci"):
                                    build_lhsT(kind, d, t, slot)
                                zr, zi, nzi = pcol(d, 2, t), pcol(d, 3, t), pcol(d, 4, t)
                                for c5 in range(0, T, 512):
                                    n5 = min(512, T - c5)
                                    pr, bpr = next_ps()
                                    pi_, bpi = next_ps()
                                    op("pe", lambda e, pr=pr, c5=c5, n5=n5, slot=slot: e.matmul(pr[:, 0:n5], LW["br"][slot][:, :], uT[:, c5:c5 + n5], start=True, stop=True),
                                       r=[b_LW["br"][slot], b_uT], w=[bpr])
                                    op("pe", lambda e, pi_=pi_, c5=c5, n5=n5, slot=slot: e.matmul(pi_[:, 0:n5], LW["bi"][slot][:, :], uT[:, c5:c5 + n5], start=True, stop=True),
                                       r=[b_LW["bi"][slot], b_uT], w=[bpi])
                                    op("dve", lambda e, pr=pr, c5=c5, n5=n5, zr=zr: e.tensor_scalar(Ar[:, c5:c5 + n5], pr[:, 0:n5], zr, None, ALU.mult),
                                       r=[bpr, b_pc[d]], w=[b_re])
                                    op("dve", lambda e, pi_=pi_, c5=c5, n5=n5, nzi=nzi: e.scalar_tensor_tensor(Ar[:, c5:c5 + n5], pi_[:, 0:n5], nzi, Ar[:, c5:c5 + n5], ALU.mult, ALU.add),
                                       r=[bpi, b_pc[d], b_re], w=[b_re])
                                    op("act", lambda e, pi_=pi_, c5=c5, n5=n5, zr=zr: e.activation(out=Ai[:, c5:c5 + n5], in_=pi_[:, 0:n5], func=AF.Identity, scale=zr),
                                       r=[bpi, b_pc[d]], w=[b_im])
                                    op("dve", lambda e, pr=pr, c5=c5, n5=n5, zi=zi: e.scalar_tensor_tensor(Ai[:, c5:c5 + n5], pr[:, 0:n5], zi, Ai[:, c5:c5 + n5], ALU.mult, ALU.add),
                                       r=[bpr, b_pc[d], b_im], w=[b_im])
                                if ss_["sample"] and "noinit" not in S5DBG:
                                    cf = 0 if d == 0 else T - 1
                                    op("dve", lambda e, cf=cf, d=d, t=t: e.tensor_tensor(Ar[:, cf:cf + 1], Ar[:, cf:cf + 1], pcol(d, 5, t), ALU.add),
                                       r=[b_re, b_pc[d]], w=[b_re])
                                    op("dve", lambda e, cf=cf, d=d, t=t: e.tensor_tensor(Ai[:, cf:cf + 1], Ai[:, cf:cf + 1], pcol(d, 6, t), ALU.add),
                                       r=[b_im, b_pc[d]], w=[b_im])
                                sR, sI, dR, dI = Ar, Ai, Br, Bi
                                for k in range(nlev):
                                    dd = 1 << k
                                    v = 7 + 3 * k
                                    pr_, pi2, npi = pcol(d, v, t), pcol(d, v + 1, t), pcol(d, v + 2, t)
                                    lastlev = (k == nlev - 1)
                                    oR, oI = (Xr, Xi) if lastlev else (dR, dI)
                                    wbufs = [b_X] if lastlev else []
                                    if d == 0:
                                        hi_o, lo_i, hd = (dd, L), (0, L - dd), (0, dd)
                                    else:
                                        hi_o, lo_i, hd = (0, L - dd), (dd, L), (L - dd, L)
                                    if lastlev and not ss_["sample"] and "nofin" not in S5DBG:
                                        cl = L - 1 if d == 0 else 0
                                        cs_ = cl - dd if d == 0 else cl + dd
                                        fr = fin[:, 0:nseq].rearrange("p (s o) -> p s o", o=1)
                                        fi = fin[:, NP:NP + nseq].rearrange("p (s o) -> p s o", o=1)
                                        op("dve", lambda e, fr=fr, sR=sR, cl=cl, cs_=cs_, pr_=pr_: e.scalar_tensor_tensor(fr, v3(sR, cs_, cs_ + 1), pr_, v3(sR, cl, cl + 1), ALU.mult, ALU.add),
                                           r=[b_re, b_im, b_pc[d]], w=[b_fin5])
                                        op("dve", lambda e, fr=fr, sI=sI, cs_=cs_, npi=npi: e.scalar_tensor_tensor(fr, v3(sI, cs_, cs_ + 1), npi, fr, ALU.mult, ALU.add),
                                           r=[b_re, b_im, b_pc[d], b_fin5], w=[b_fin5])
                                        op("dve", lambda e, fi=fi, sI=sI, cl=cl, cs_=cs_, pr_=pr_: e.scalar_tensor_tensor(fi, v3(sI, cs_, cs_ + 1), pr_, v3(sI, cl, cl + 1), ALU.mult, ALU.add),
                                           r=[b_re, b_im, b_pc[d], b_fin5], w=[b_fin5])
                                        op("dve", lambda e, fi=fi, sR=sR, cs_=cs_, pi2=pi2: e.scalar_tensor_tensor(fi, v3(sR, cs_, cs_ + 1), pi2, fi, ALU.mult, ALU.add),
                                           r=[b_re, b_im, b_pc[d], b_fin5], w=[b_fin5])
                                        for sq_ in range(nseq):
                                            dma(out_s5re[sq_, l, d, :, :].rearrange("g p -> (g p)")[t * 128:(t + 1) * 128].rearrange("(q o) -> q o", o=1),
                                                fin[:, sq_:sq_ + 1], r=[b_fin5], q="pool")
                                            dma(out_s5im[sq_, l, d, :, :].rearrange("g p -> (g p)")[t * 128:(t + 1) * 128].rearrange("(q o) -> q o", o=1),
                                                fin[:, NP + sq_:NP + sq_ + 1], r=[b_fin5], q="pool")
                                    op("dve", lambda e, oR=oR, sR=sR, hi_o=hi_o, lo_i=lo_i, pr_=pr_: e.scalar_tensor_tensor(v3(oR, *hi_o), v3(sR, *lo_i), pr_, v3(sR, *hi_o), ALU.mult, ALU.add),
                                       r=[b_re, b_im, b_pc[d]], w=[b_re] + wbufs)
                                    op("dve", lambda e, oR=oR, sI=sI, hi_o=hi_o, lo_i=lo_i, npi=npi: e.scalar_tensor_tensor(v3(oR, *hi_o), v3(sI, *lo_i), npi, v3(oR, *hi_o), ALU.mult, ALU.add),
                                       r=[b_re, b_im, b_pc[d]] + wbufs, w=[b_re] + wbufs)
                                    op("act", lambda e, oR=oR, sR=sR, hd=hd: e.activation(out=v3(oR, *hd), in_=v3(sR, *hd), func=AF.Copy),
                                       r=[b_re], w=[b_re] + wbufs)
                                    op("dve", lambda e, oI=oI, sI=sI, hi_o=hi_o, lo_i=lo_i, pr_=pr_: e.scalar_tensor_tensor(v3(oI, *hi_o), v3(sI, *lo_i), pr_, v3(sI, *hi_o), ALU.mult, ALU.add),
                                       r=[b_re, b_im, b_pc[d]], w=[b_im] + wbufs)
                                    op("dve", lambda e, oI=oI, sR=sR, hi_o=hi_o, lo_i=lo_i, pi2=pi2: e.scalar_tensor_tensor(v3(oI, *hi_o), v3(sR, *lo_i), pi2, v3(oI, *hi_o), ALU.mult, ALU.add),
                                       r=[b_re, b_im, b_pc[d]] + wbufs, w=[b_im] + wbufs)
                                    op("act", lambda e, oI=oI, sI=sI, hd=hd: e.activation(out=v3(oI, *hd), in_=v3(sI, *hd), func=AF.Copy),
                                       r=[b_im], w=[b_im] + wbufs)
                                    sR, sI, dR, dI = dR, dI, sR, sI
                                for c5 in range(0, T, 512):
                                    n5 = min(512, T - c5)
                                    py, bpy = next_ps()
                                    op("pe", lambda e, py=py, c5=c5, n5=n5, slot=slot: e.matmul(py[:, 0:n5], LW["cr"][slot][:, :], Xr[:, c5:c5 + n5], start=True, stop=False),
                                       r=[b_LW["cr"][slot], b_X], w=[bpy], sig=False)
                                    op("pe", lambda e, py=py, c5=c5, n5=n5, slot=slot: e.matmul(py[:, 0:n5], LW["ci"][slot][:, :], Xi[:, c5:c5 + n5], start=False, stop=True),
                                       r=[b_LW["ci"][slot], b_X], w=[bpy])
                                    if first_acc:
                                        op("dve", lambda e, py=py, c5=c5, n5=n5: e.tensor_copy(yacc[:, c5:c5 + n5], py[:, 0:n5]), r=[bpy], w=[b_yacc])
                                    else:
                                        op("dve", lambda e, py=py, c5=c5, n5=n5: e.tensor_tensor(yacc[:, c5:c5 + n5], py[:, 0:n5], yacc[:, c5:c5 + n5], ALU.add),
                                           r=[bpy, b_yacc], w=[b_yacc])
                                first_acc = False
                        g1, g2 = Br, Bi
                        op("dve", lambda e, j8=j8: e.scalar_tensor_tensor(yacc[:, 0:T], uT[:, 0:T], dcol[:, j8:j8 + 1], yacc[:, 0:T], ALU.mult, ALU.add),
                           r=[b_uT, b_dcol, b_yacc], w=[b_yacc])
                        op("pool", lambda e: e.tensor_tensor(g1[:, 0:T], yacc[:, 0:T], yacc[:, 0:T], ALU.mult), r=[b_yacc, b_re], w=[b_re])
                        op("dve", lambda e: e.tensor_scalar(g1[:, 0:T], g1[:, 0:T], 0.044715, 1.0, ALU.mult, ALU.add), r=[b_re], w=[b_re])
                        op("pool", lambda e: e.tensor_tensor(g1[:, 0:T], g1[:, 0:T], yacc[:, 0:T], ALU.mult), r=[b_yacc, b_re], w=[b_re])
                        op("act", lambda e: e.activation(out=g2[:, 0:T], in_=g1[:, 0:T], func=AF.Sigmoid, scale=1.5957691216057308), r=[b_re, b_im], w=[b_im])
                        op("dve", lambda e: e.tensor_tensor(yab[:, 0:T], yacc[:, 0:T], g2[:, 0:T], ALU.mult), r=[b_yacc, b_im], w=[b_yab])
                        dma(yaT[j8 * 128:(j8 + 1) * 128, t0:t0 + T], yab[:, 0:T], r=[b_yab], q="pool")
            cx.barrier()
            if "noglu" in cfg.get("S5DBG", ()):
                return
            with ExitStack() as es2:
                gbcol = es2.enter_context(sbt("s5gb2", [128, NU], F32))
                b_gb = cx.buf()
                colize(es2, s5_glu_b[l:l + 1, :], S5W, gbcol, 0, b_gb, "s5gb2")
                gw = [es2.enter_context(sbt("s5gw%d" % i, [128, S5W], BF16)) for i in range(NU)]
                b_gw = cx.buf()
                for jt in range(NU):
                    dma(gw[jt][:, :], glu_t[l][jt, :, :], w=[b_gw])
                ya = [es2.enter_context(sbt("s5ya%d" % i, [128, NU, 512], BF16)) for i in range(2)]
                b_ya = [cx.buf(), cx.buf()]
                sgt = [es2.enter_context(sbt("s5sg%d" % i, [128, 512], F32)) for i in range(2)]
                ob = [es2.enter_context(sbt("s5ob%d" % i, [128, 512], BF16)) for i in range(2)]
                b_sg = [cx.buf(), cx.buf()]
                b_ob = [cx.buf(), cx.buf()]
                yv = yaT.rearrange("(kt p) t -> p kt t", p=128)
                bi_ = 0
                for c5 in range(0, TOK, 512):
                    n5 = min(512, TOK - c5)
                    i = bi_ % 2
                    bi_ += 1
                    dma(ya[i][:, :, 0:n5], yv[:, :, c5:c5 + n5], w=[b_ya[i]])
                    for jt in range(NU):
                        pg, bpg = next_ps()
                        for kt in range(NU):
                            op("pe", lambda e, pg=pg, jt=jt, kt=kt, i=i, n5=n5: e.matmul(pg[:, 0:n5], gw[jt][:, kt * 128:(kt + 1) * 128], ya[i][:, kt, 0:n5],
                                                                                       start=(kt == 0), stop=(kt == NU - 1)),
                               r=[b_gw, b_ya[i]], w=[bpg], sig=(kt == NU - 1))
                        s = jt % 2
                        op("act", lambda e, pg=pg, s=s, jt=jt, n5=n5: e.activation(out=sgt[s][:, 0:n5], in_=pg[:, 0:n5], func=AF.Sigmoid, bias=gbcol[:, jt:jt + 1], scale=1.0),
                           r=[bpg, b_gb], w=[b_sg[s]])
                        op("dve", lambda e, s=s, jt=jt, i=i, n5=n5: e.tensor_tensor(ob[s][:, 0:n5], sgt[s][:, 0:n5], ya[i][:, jt, 0:n5], ALU.mult),
                           r=[b_sg[s], b_ya[i]], w=[b_ob[s]])
                        dma(mixedT[jt * 128:(jt + 1) * 128, c5:c5 + n5], ob[s][:, 0:n5], r=[b_ob[s]], q="pool")
            cx.barrier()

        for l in range(DEPTH):
            ada_phase(l)
            ffn_phase(l, "ffn1", 0)
            if MIX:
                SKIP = cfg.get("SKIP", ())
                win_phase(l)
                if "s5" not in SKIP:
                    s5_phase(l)
                if "gla" not in SKIP:
                    lin_attn_phase(l, "gla")
                if "hgrn" not in SKIP:
                    lin_attn_phase(l, "hgrn")
                if "wout" not in SKIP:
                    wout_phase(l)
            ffn_phase(l, "ffn2", 2)
        final_phase()
        cx.barrier()
    return nc


_CONST = {"c_ident": np.eye(128, dtype=np.float32),
          "c_triu": np.triu(np.ones((128, 128), dtype=np.float32)),
          "c_tril": np.tril(np.ones((128, 128), dtype=np.float32))}
_W_NAMES = ("ada_w", "ada_b", "norm_g", "ffn1_wg", "ffn1_wu", "ffn1_wd", "ffn2_wg", "ffn2_wu", "ffn2_wd",
            "w_in", "w_out", "final_norm_g", "gla_w2", "gla_b2", "gla_norm_g", "hg_lb_raw", "hg_norm_g",
            "s5_a_re", "s5_a_im", "s5_log_dt", "s5_b_re", "s5_b_im", "s5_c_re", "s5_c_im", "s5_glu_w", "s5_glu_b")


def run_cfg(cfg, inputs, n_cores=8):
    NP, SEQ, LS, D = cfg["NP"], cfg["SEQ"], cfg["LS"], cfg["D"]
    f32 = np.float32
    xp = np.asarray(inputs["x_prompt"], f32)
    xs = np.asarray(inputs["x_sample"], f32)
    nb = xs.shape[0]
    nc = build_program(cfg)
    in_maps = []
    for i in range(n_cores):
        b = i % nb
        m = {}
        m["x_in"] = np.ascontiguousarray(np.concatenate([xp[i * NP:(i + 1) * NP].reshape(NP * SEQ, D), xs[b]], axis=0))
        m["cond"] = np.ascontiguousarray(np.stack([np.asarray(inputs["c_ctx"], f32), np.asarray(inputs["c"], f32)[b]], axis=0))
        for nm in _W_NAMES:
            m[nm] = np.ascontiguousarray(np.asarray(inputs[nm], f32))
        m["st_s5re"] = np.ascontiguousarray(np.asarray(inputs["state_s5_re"], f32)[b].reshape(cfg["DEPTH"], 2, -1))
        m["st_s5im"] = np.ascontiguousarray(np.asarray(inputs["state_s5_im"], f32)[b].reshape(cfg["DEPTH"], 2, -1))
        m["s5_d"] = np.ascontiguousarray(np.asarray(inputs["s5_d"], f32).reshape(cfg["DEPTH"], -1))
        m["st_gla"] = np.ascontiguousarray(np.asarray(inputs["state_gla"], f32)[b])
        m["st_hgrn"] = np.ascontiguousarray(np.asarray(inputs["state_hgrn"], f32)[b])
        m.update(_CONST)
        in_maps.append(m)
    res = run_bass_kernel_spmd(nc, in_maps, core_ids=list(range(n_cores)))
    R = res.results
    y_prompt = np.concatenate([R[i]["y_out"][:NP * SEQ].reshape(NP, SEQ, D) for i in range(n_cores)], axis=0)
    y_sample = np.stack([R[b]["y_out"][NP * SEQ:] for b in range(nb)], axis=0)
    gla = np.concatenate([R[i]["out_gla"] for i in range(n_cores)], axis=0) if "out_gla" in R[0] else None
    hgrn = np.concatenate([R[i]["out_hgrn"] for i in range(n_cores)], axis=0) if "out_hgrn" in R[0] else None
    s5re = np.concatenate([R[i]["out_s5re"] for i in range(n_cores)], axis=0) if "out_s5re" in R[0] else None
    s5im = np.concatenate([R[i]["out_s5im"] for i in range(n_cores)], axis=0) if "out_s5im" in R[0] else None
    return y_prompt, y_sample, s5re, s5im, gla, hgrn


def kernel(**inputs):
    cfg = FULL_CFG
    return run_cfg(cfg, inputs)
```
